# Optimizing a Trainium2 kernel written in Bass

```python
import math
import jax, jax.numpy as jnp
from jax import lax
import numpy as np

D_MODEL = 1024
BATCH = 2
SEQ = 8192
DEPTH = 1
DEC_BATCH = 4
DEC_SEQ = 8192
PAST_LEN = 128

PLE_DIM = 256
GRID_W = 64
Q_BLOCK = 128
EPS = 1e-6
A_HEADS = 8
A_HEAD_DIM = 64
A_ROT_DIM = A_HEAD_DIM // 4
A_ROPE_THETA = 500000.0
B_HEADS = 16
B_KV_HEADS = 4
B_HEAD_DIM = 64
B_AXIAL_THETA = 10000.0
D_FF = 2816
CONV_W = 3

A_Q = A_HEADS * 2 * A_HEAD_DIM
A_K = A_HEADS * 2 * A_HEAD_DIM
A_V = A_HEADS * 2 * A_HEAD_DIM
B_Q = B_HEADS * B_HEAD_DIM
B_KV = B_KV_HEADS * B_HEAD_DIM
A_WIDTH = A_V
B_WIDTH = B_Q
IN_COLS = A_Q + A_K + A_V + B_Q + 2 * B_KV + 2 * D_MODEL
IN_OFFSETS = (A_Q, A_Q + A_K, A_Q + A_K + A_V, A_Q + A_K + A_V + B_Q,
              A_Q + A_K + A_V + B_Q + B_KV, A_Q + A_K + A_V + B_Q + 2 * B_KV,
              A_Q + A_K + A_V + B_Q + 2 * B_KV + D_MODEL)

kernel_name = "hybrid_diffattn_axial_gqa_encoder"


def rms_norm(x, g):
    x32 = x.astype(jnp.float32)
    y = x32 * lax.rsqrt(jnp.mean(x32 * x32, axis=-1, keepdims=True) + EPS)
    return (y * g.astype(jnp.float32)).astype(x.dtype)


def rope_tables(pos, dim, theta):
    inv = theta ** (-jnp.arange(0, dim, 2, dtype=jnp.float32) / dim)
    ang = pos.astype(jnp.float32)[:, None] * inv[None, :]
    return jnp.cos(ang), jnp.sin(ang)


def rotate(x, cos, sin):
    half = x.shape[-1] // 2
    x1, x2 = x[..., :half], x[..., half:]
    c = cos[None, :, None, :].astype(x.dtype)
    s = sin[None, :, None, :].astype(x.dtype)
    return jnp.concatenate([x1 * c - x2 * s, x2 * c + x1 * s], axis=-1)


def partial_rope(x, cos, sin):
    rd = 2 * cos.shape[-1]
    return jnp.concatenate([rotate(x[..., :rd], cos, sin), x[..., rd:]], axis=-1)


def axial_rope(x, cos_r, sin_r, cos_c, sin_c):
    half = x.shape[-1] // 2
    return jnp.concatenate([rotate(x[..., :half], cos_r, sin_r),
                            rotate(x[..., half:], cos_c, sin_c)], axis=-1)


def to_blocks(x):
    b, s = x.shape[:2]
    return jnp.moveaxis(x.reshape((b, s // Q_BLOCK, Q_BLOCK) + x.shape[2:]), 1, 0)


def from_blocks(y):
    nb, b, qb = y.shape[:3]
    return jnp.moveaxis(y, 0, 1).reshape((b, nb * qb) + y.shape[3:])


def diff_attention(q1, q2, k1, k2, v, lam):
    scale = A_HEAD_DIM ** -0.5

    def block(qb):
        q1b, q2b = qb
        s1 = jnp.einsum('bqhd,bkhd->bhqk', q1b, k1).astype(jnp.float32) * scale
        s2 = jnp.einsum('bqhd,bkhd->bhqk', q2b, k2).astype(jnp.float32) * scale
        a = jax.nn.softmax(s1, axis=-1) - lam * jax.nn.softmax(s2, axis=-1)
        return jnp.einsum('bhqk,bkhe->bqhe', a.astype(v.dtype), v)

    return from_blocks(lax.map(block, (to_blocks(q1), to_blocks(q2))))


def gqa_attention(q, k, v):
    b, s = q.shape[:2]
    rep = B_HEADS // B_KV_HEADS
    scale = B_HEAD_DIM ** -0.5
    qg = q.reshape(b, s, B_KV_HEADS, rep, B_HEAD_DIM)

    def block(qb):
        sc = jnp.einsum('bqgrd,bkgd->bgrqk', qb, k).astype(jnp.float32) * scale
        pr = jax.nn.softmax(sc, axis=-1)
        return jnp.einsum('bgrqk,bkgd->bqgrd', pr.astype(v.dtype), v)

    out = from_blocks(lax.map(block, to_blocks(qg)))
    return out.reshape(b, s, B_WIDTH)


def depthwise_conv_centred(u, w, bias):
    s = u.shape[1]
    up = jnp.pad(u, ((0, 0), (1, 1), (0, 0)))
    return up[:, :s] * w[0] + up[:, 1:s + 1] * w[1] + up[:, 2:s + 2] * w[2] + bias


def encode(x, p, g_mix, w_in, lambda_q1, lambda_k1, lambda_q2, lambda_k2, g_diff, w_a,
           g_qn, g_kn, w_b, w_out, g_ffn, w_up, conv_w, conv_b, w_down,
           w_ple, g_ple, w_ple_gate, g_final):
    b, s, _ = x.shape
    rows = s // GRID_W
    pos = jnp.arange(s, dtype=jnp.int32)
    row_ids = jnp.repeat(jnp.arange(rows, dtype=jnp.int32), GRID_W)
    col_ids = jnp.tile(jnp.arange(GRID_W, dtype=jnp.int32), rows)
    cos_a, sin_a = rope_tables(pos, A_ROT_DIM, A_ROPE_THETA)
    cos_r, sin_r = rope_tables(row_ids, B_HEAD_DIM // 2, B_AXIAL_THETA)
    cos_c, sin_c = rope_tables(col_ids, B_HEAD_DIM // 2, B_AXIAL_THETA)

    h = x
    for i in range(DEPTH):
        lam_init = 0.8 - 0.6 * math.exp(-0.3 * i)
        n = rms_norm(h, g_mix[i])
        proj = n @ w_in[i]
        aq, ak, av, bq, bk, bv, ga, gb = jnp.split(proj, IN_OFFSETS, axis=-1)

        aq = aq.reshape(b, s, A_HEADS, 2, A_HEAD_DIM)
        ak = ak.reshape(b, s, A_HEADS, 2, A_HEAD_DIM)
        av = av.reshape(b, s, A_HEADS, 2 * A_HEAD_DIM)
        q1 = partial_rope(aq[..., 0, :], cos_a, sin_a)
        q2 = partial_rope(aq[..., 1, :], cos_a, sin_a)
        k1 = partial_rope(ak[..., 0, :], cos_a, sin_a)
        k2 = partial_rope(ak[..., 1, :], cos_a, sin_a)
        lam = (jnp.exp(jnp.sum(lambda_q1[i].astype(jnp.float32) * lambda_k1[i].astype(jnp.float32)))
               - jnp.exp(jnp.sum(lambda_q2[i].astype(jnp.float32) * lambda_k2[i].astype(jnp.float32)))
               + lam_init)
        oa = diff_attention(q1, q2, k1, k2, av, lam)
        oa = rms_norm(oa, g_diff[i]) * (1.0 - lam_init)
        ya = oa.reshape(b, s, A_WIDTH) @ w_a[i]

        bq = axial_rope(rms_norm(bq.reshape(b, s, B_HEADS, B_HEAD_DIM), g_qn[i]),
                        cos_r, sin_r, cos_c, sin_c)
        bk = axial_rope(rms_norm(bk.reshape(b, s, B_KV_HEADS, B_HEAD_DIM), g_kn[i]),
                        cos_r, sin_r, cos_c, sin_c)
        bv = bv.reshape(b, s, B_KV_HEADS, B_HEAD_DIM)
        yb = gqa_attention(bq, bk, bv) @ w_b[i]

        merged = jax.nn.sigmoid(ga) * ya + jax.nn.sigmoid(gb) * yb
        h = h + merged @ w_out[i]

        n2 = rms_norm(h, g_ffn[i])
        u = depthwise_conv_centred(n2 @ w_up[i], conv_w[i], conv_b[i])
        ug, uv = jnp.split(u, 2, axis=-1)
        h = h + (jax.nn.gelu(ug) * uv) @ w_down[i]

        gate = jax.nn.sigmoid(rms_norm(h, g_ple[i]) @ w_ple_gate[i])
        h = h + (p[i] @ w_ple[i]) * gate
    return rms_norm(h, g_final)


def setup_inputs(seed: int = 0) -> dict:
    key = jax.random.key(seed)
    ks = jax.random.split(key, 32)
    f32 = jnp.float32

    def dense(k, shape, fan_in):
        return jax.random.normal(k, shape, f32) * (fan_in ** -0.5)

    def gain(k, shape):
        return 1.0 + 0.01 * jax.random.normal(k, shape, f32)

    return {
        "x_prompt": jax.random.normal(ks[0], (BATCH, SEQ, D_MODEL), f32),
        "x_sample": jax.random.normal(ks[1], (DEC_BATCH, DEC_SEQ, D_MODEL), f32),
        "p_prompt": jax.random.normal(ks[2], (DEPTH, BATCH, SEQ, PLE_DIM), f32),
        "p_sample": jax.random.normal(ks[3], (DEPTH, DEC_BATCH, DEC_SEQ, PLE_DIM), f32),
        "g_mix": gain(ks[4], (DEPTH, D_MODEL)),
        "w_in": dense(ks[5], (DEPTH, D_MODEL, IN_COLS), D_MODEL),
        "lambda_q1": 0.1 * jax.random.normal(ks[6], (DEPTH, A_HEAD_DIM), f32),
        "lambda_k1": 0.1 * jax.random.normal(ks[7], (DEPTH, A_HEAD_DIM), f32),
        "lambda_q2": 0.1 * jax.random.normal(ks[8], (DEPTH, A_HEAD_DIM), f32),
        "lambda_k2": 0.1 * jax.random.normal(ks[9], (DEPTH, A_HEAD_DIM), f32),
        "g_diff": gain(ks[10], (DEPTH, 2 * A_HEAD_DIM)),
        "w_a": dense(ks[11], (DEPTH, A_WIDTH, D_MODEL), A_WIDTH),
        "g_qn": gain(ks[12], (DEPTH, B_HEAD_DIM)),
        "g_kn": gain(ks[13], (DEPTH, B_HEAD_DIM)),
        "w_b": dense(ks[14], (DEPTH, B_WIDTH, D_MODEL), B_WIDTH),
        "w_out": dense(ks[15], (DEPTH, D_MODEL, D_MODEL), D_MODEL),
        "g_ffn": gain(ks[16], (DEPTH, D_MODEL)),
        "w_up": dense(ks[17], (DEPTH, D_MODEL, 2 * D_FF), D_MODEL),
        "conv_w": dense(ks[18], (DEPTH, CONV_W, 2 * D_FF), CONV_W),
        "conv_b": 0.01 * jax.random.normal(ks[19], (DEPTH, 2 * D_FF), f32),
        "w_down": dense(ks[20], (DEPTH, D_FF, D_MODEL), D_FF),
        "w_ple": dense(ks[21], (DEPTH, PLE_DIM, D_MODEL), PLE_DIM),
        "g_ple": gain(ks[22], (DEPTH, D_MODEL)),
        "w_ple_gate": dense(ks[23], (DEPTH, D_MODEL, D_MODEL), D_MODEL),
        "g_final": gain(ks[24], (D_MODEL,)),
    }


def reference(x_prompt, x_sample, p_prompt, p_sample, g_mix, w_in, lambda_q1, lambda_k1,
              lambda_q2, lambda_k2, g_diff, w_a, g_qn, g_kn, w_b, w_out, g_ffn, w_up,
              conv_w, conv_b, w_down, w_ple, g_ple, w_ple_gate, g_final):
    y_prompt = encode(x_prompt, p_prompt, g_mix, w_in, lambda_q1, lambda_k1, lambda_q2, lambda_k2,
                      g_diff, w_a, g_qn, g_kn, w_b, w_out, g_ffn, w_up, conv_w, conv_b, w_down,
                      w_ple, g_ple, w_ple_gate, g_final)
    y_sample = encode(x_sample, p_sample, g_mix, w_in, lambda_q1, lambda_k1, lambda_q2, lambda_k2,
                      g_diff, w_a, g_qn, g_kn, w_b, w_out, g_ffn, w_up, conv_w, conv_b, w_down,
                      w_ple, g_ple, w_ple_gate, g_final)
    return (y_prompt, y_sample)
```

```python
import math
from contextlib import ExitStack

import numpy as np
import concourse.bass as bass
import concourse.mybir as mybir
from concourse.bass_utils import run_bass_kernel_spmd

F32 = mybir.dt.float32
BF16 = mybir.dt.bfloat16
AF = mybir.ActivationFunctionType
ALU = mybir.AluOpType
AX = mybir.AxisListType

D = 1024
DFF = 2816
NFC = DFF // 128
EPS = 1e-6
LAM_INIT = 0.8 - 0.6 * math.exp(-0.3 * 0)
HW = 32
_STOP = 99
_NOPOOL = False
_SKIP = set()
_DUMP = set()
_LAST = {}


_UID = [0]


def _uniq(name):
    _UID[0] += 1
    return "%s_%d" % (name, _UID[0])


class Sched:
    def __init__(self, nc, stack):
        self.nc = nc
        self.stack = stack
        self.eng = {"pe": nc.tensor, "act": nc.scalar, "dve": nc.vector, "pool": nc.gpsimd, "sp": nc.sync}
        self.sems = {}
        self.count = {}
        self.seen = {e: {} for e in self.eng}
        self.lastw = {}
        self.readers = {}
        self.rr = 0
        self.dma_slot = {}
        self.epoch = 0
        self.nbank = 0
        self.nhalf = 0

    def _sem(self, name):
        if name not in self.sems:
            self.sems[name] = self.stack.enter_context(self.nc.semaphore(name))
            self.count[name] = 0
        return self.sems[name]

    def new_epoch(self):
        self.epoch += 1

    def op(self, e, fn, reads=(), writes=(), sig=True, dma=None):
        if e == "pool" and _NOPOOL:
            e = "dve"
        xr = [r for r in reads if r.startswith(("ps", "S", "Ob"))]
        if xr:
            writes = list(writes) + xr
        own = "e%d_%s" % (self.epoch, e)
        waits = {}

        def need(tok):
            if tok is None:
                return
            s, v = tok
            if s == own and e == "pe":
                return
            if self.seen[e].get(s, 0) >= v:
                return
            waits[s] = max(waits.get(s, 0), v)

        for r in reads:
            need(self.lastw.get(r))
        for w in writes:
            need(self.lastw.get(w))
            for s, v in self.readers.get(w, {}).items():
                need((s, v))
        eng = self.eng[e]
        for s, v in waits.items():
            assert v <= self.count[s], ("wait on unsignalled", e, s, v, self.count[s])
            eng.wait_ge(self.sems[s], v)
            self.seen[e][s] = v
        ins = fn(eng)
        if dma is not None:
            s = "d%d" % self.dma_slot.setdefault(dma, len(self.dma_slot))
            h = self._sem(s)
            assert self.count[s] < 30000
            self.count[s] += 16
            ins.then_inc(h, 16)
            tok = (s, self.count[s])
        else:
            h = self._sem(own)
            if sig:
                self.count[own] += 1
                ins.then_inc(h, 1)
                tok = (own, self.count[own])
            else:
                tok = (own, self.count[own] + 1)
        for r in reads:
            d = self.readers.setdefault(r, {})
            d[tok[0]] = max(d.get(tok[0], 0), tok[1])
        for w in writes:
            self.lastw[w] = tok
            self.readers[w] = {}
        return tok

    def barrier(self):
        for e, eng in self.eng.items():
            for s, c in self.count.items():
                if c > 0 and self.seen[e].get(s, 0) < c and not (s == "e%d_%s" % (self.epoch, e)):
                    eng.wait_ge(self.sems[s], c)
                    self.seen[e][s] = c
        self.lastw = {}
        self.readers = {}
        self.dma_slot = {}

    def bank(self):
        b = self.nbank % 6
        self.nbank += 1
        return b

    def half(self):
        h = self.nhalf % 2
        self.nhalf += 1
        return h

    def pick(self, *engs):
        self.rr += 1
        return engs[self.rr % len(engs)]


def build(S, QN, NU):
    NKB = S // 128
    QT = QN + HW
    NSUBQ = QN // 128
    nc = bass.Bass("TRN2", target_bir_lowering=False)

    def din(name, shape):
        return nc.dram_tensor(name, list(shape), F32, kind="ExternalInput").ap()

    def dscr(name, shape, dt=BF16):
        return nc.dram_tensor(name, list(shape), dt, kind=("ExternalOutput" if name in _DUMP else "Internal")).ap()

    xs = din("xs", [NU, S, D])
    xq = din("xq", [NU, QT, D])
    pq = din("pq", [NU, QN, 256])
    tabAk = din("tabAk", [S, 16])
    tabBk = din("tabBk", [S, 128])
    tabAq = din("tabAq", [NU, QT, 16])
    tabBq = din("tabBq", [NU, QT, 128])
    hmask = din("hmask", [1, 2 * NU])
    ident = din("ident", [128, 128])
    g_mix = din("g_mix", [1, D])
    w_in = din("w_in", [D, 6656])
    lq1 = din("lambda_q1", [1, 64])
    lk1 = din("lambda_k1", [1, 64])
    lq2 = din("lambda_q2", [1, 64])
    lk2 = din("lambda_k2", [1, 64])
    g_diff = din("g_diff", [1, 128])
    w_a = din("w_a", [D, D])
    g_qn = din("g_qn", [1, 64])
    g_kn = din("g_kn", [1, 64])
    w_b = din("w_b", [D, D])
    w_out = din("w_out", [D, D])
    g_ffn = din("g_ffn", [1, D])
    w_up = din("w_up", [D, 2 * DFF])
    conv_w = din("conv_w", [3, 2 * DFF])
    conv_b = din("conv_b", [1, 2 * DFF])
    w_down = din("w_down", [DFF, D])
    w_ple = din("w_ple", [256, D])
    g_ple = din("g_ple", [1, D])
    w_pg = din("w_ple_gate", [D, D])
    g_final = din("g_final", [1, D])
    yout = nc.dram_tensor("y", [NU, QN, D], F32, kind="ExternalOutput").ap()

    Wb_in = dscr("Wb_in", [D, 6656])
    Wb_a = dscr("Wb_a", [D, D])
    Wb_b = dscr("Wb_b", [D, D])
    Wb_out = dscr("Wb_out", [D, D])
    Wb_up = dscr("Wb_up", [D, 2 * DFF])
    Wb_down = dscr("Wb_down", [DFF, D])
    Wb_ple = dscr("Wb_ple", [256, D])
    Wb_pg = dscr("Wb_pg", [D, D])
    KTA = dscr("KTA", [8, 128, S])
    KTB = dscr("KTB", [2, 128, S])
    VA = dscr("VA", [8, S, 128])
    VB = dscr("VB", [4, S, 64])
    QTA = dscr("QTA", [8, 128, QT])
    QTB = dscr("QTB", [2, 128, 4 * QT])
    NT = dscr("NT", [8, 128, QT])
    OT = dscr("OT", [16, 128, QT])
    H2 = dscr("H2", [QN, D], F32)
    ACTS = dscr("ACTS", [NFC, 128, QN])

    with ExitStack() as stack:
        sc = Sched(nc, stack)
        op = sc.op

        def sb(name, shape, dt=F32):
            return stack.enter_context(nc.sbuf_tensor(_uniq(name), list(shape), dt))

        PS = stack.enter_context(nc.psum_tensor("PS", [128, 6, 512], F32))
        PSB = stack.enter_context(nc.psum_tensor("PSB", [128, 2, 1024], BF16))

        def psb_half(h):
            return PSB[:, h, 0:512].rearrange("p (a b) -> p a b", a=4)

        def dma(out, in_, key, reads=(), writes=()):
            return op("sp", lambda e: e.dma_start(out=out, in_=in_), reads=reads, writes=writes, dma=key)

        identf = sb("identf", [128, 128])
        identb = sb("identb", [128, 128], BF16)
        gq_t = sb("gq_t", [128, 64])
        gk_t = sb("gk_t", [128, 64])
        gqsw = sb("gqsw", [128, 64])
        gksw = sb("gksw", [128, 64])
        gd_t = sb("gd_t", [128, 128])
        lamv = sb("lamv", [128, 4, 64])
        lams = sb("lams", [128, 8])
        hm_t = sb("hm_t", [128, 2 * NU])
        cwt = sb("cwt", [128, 4, 2 * NFC])
        onesb = sb("onesb", [128, 128], BF16)
        gdc = sb("gdc", [128, 1])

        dma(identf[:], ident[:, :], "c0", writes=["identf"])
        op("dve", lambda e: e.tensor_copy(out=identb[:], in_=identf[:]), reads=["identf"], writes=["identb"])
        for t, src, nm in ((gq_t, g_qn, "gq"), (gk_t, g_kn, "gk"),
                           (gd_t, g_diff, "gd"), (hm_t, hmask, "hm")):
            dma(t[:], src.partition_broadcast(128), "c_" + nm, writes=[nm])
        for i, src in enumerate((lq1, lk1, lq2, lk2)):
            dma(lamv[:, i, :], src.partition_broadcast(128), "c_l%d" % i, writes=["lamv%d" % i])
        for gt, gs, nm in ((gq_t, gqsw, "gq"), (gk_t, gksw, "gk")):
            for a, b in ((0, 16), (16, 0), (32, 48), (48, 32)):
                op("dve", lambda e, gs=gs, gt=gt, a=a, b=b: e.tensor_copy(out=gs[:, a:a + 16], in_=gt[:, b:b + 16]),
                   reads=[nm], writes=[nm + "sw"])
        op("dve", lambda e: e.tensor_tensor(out=lamv[:, 0, :], in0=lamv[:, 0, :], in1=lamv[:, 1, :], op=ALU.mult),
           reads=["lamv0", "lamv1"], writes=["lamv0"])
        op("dve", lambda e: e.tensor_tensor(out=lamv[:, 2, :], in0=lamv[:, 2, :], in1=lamv[:, 3, :], op=ALU.mult),
           reads=["lamv2", "lamv3"], writes=["lamv2"])
        op("dve", lambda e: e.tensor_reduce(out=lams[:, 0:1], in_=lamv[:, 0, :], axis=AX.X, op=ALU.add),
           reads=["lamv0"], writes=["lams"])
        op("dve", lambda e: e.tensor_reduce(out=lams[:, 1:2], in_=lamv[:, 2, :], axis=AX.X, op=ALU.add),
           reads=["lamv2", "lams"], writes=["lams"])
        op("act", lambda e: e.activation(out=lams[:, 2:4], in_=lams[:, 0:2], func=AF.Exp), reads=["lams"], writes=["lams"])
        op("dve", lambda e: e.tensor_tensor(out=lams[:, 4:5], in0=lams[:, 2:3], in1=lams[:, 3:4], op=ALU.subtract),
           reads=["lams"], writes=["lams"])
        op("dve", lambda e: e.tensor_scalar(out=lams[:, 5:6], in0=lams[:, 4:5], scalar1=float(LAM_INIT), scalar2=None,
                                            op0=ALU.add), reads=["lams"], writes=["lams"])
        lam_ap = lams[:, 5:6]
        op("dve", lambda e: e.tensor_scalar(out=lams[:, 6:7], in0=lams[:, 5:6], scalar1=-1.0, scalar2=None, op0=ALU.mult),
           reads=["lams"], writes=["lams"])
        op("pool", lambda e: e.memset(onesb[:], 1.0), writes=["onesb"])
        dma(gdc[:], g_diff.rearrange("o d -> d o"), "c_gdc", writes=["gdc"])
        op("dve", lambda e: e.tensor_scalar(out=gdc[:], in0=gdc[:], scalar1=float(1.0 - LAM_INIT), scalar2=None, op0=ALU.mult),
           reads=["gdc"], writes=["gdc"])
        op("dve", lambda e: e.tensor_scalar(out=gd_t[:], in0=gd_t[:], scalar1=float(1.0 - LAM_INIT), scalar2=None,
                                            op0=ALU.mult), reads=["gd"], writes=["gd"])

        with ExitStack() as ph:
            def psb_(name, shape, dt=F32):
                return ph.enter_context(nc.sbuf_tensor(_uniq(name), list(shape), dt))
            stf = [psb_("stf%d" % i, [128, 2048]) for i in range(4)]
            stb = [psb_("stb%d" % i, [128, 2048], BF16) for i in range(4)]
            cwb = psb_("cwb", [2 * NFC, 4, 128])
            for j in range(3):
                dma(cwb[:, j, :], conv_w[j].rearrange("(c p) -> c p", p=128), "cw%d" % j, writes=["cwb"])
            dma(cwb[:, 3, :], conv_b[0].rearrange("(c p) -> c p", p=128), "cw3", writes=["cwb"])
            for j in range(4):
                bk = sc.bank()
                op("pe", lambda e, j=j, bk=bk: e.transpose(PS[:, bk, 0:2 * NFC], cwb[:, j, :], identf[0:2 * NFC, 0:2 * NFC]),
                   reads=["cwb", "identf"], writes=["ps%d" % bk])
                op("dve", lambda e, j=j, bk=bk: e.tensor_copy(out=cwt[:, j, :], in_=PS[:, bk, 0:2 * NFC]),
                   reads=["ps%d" % bk], writes=["cwt"])
            pieces = []
            for src, dst in ((w_in, Wb_in), (w_a, Wb_a), (w_b, Wb_b), (w_out, Wb_out), (w_up, Wb_up),
                             (w_down, Wb_down), (w_ple, Wb_ple), (w_pg, Wb_pg)):
                R, C = src.shape
                for r0 in range(0, R, 128):
                    for c0 in range(0, C, 2048):
                        pieces.append((src, dst, r0, c0, min(2048, C - c0)))

            def p0_load(i):
                src, dst, r0, c0, cw_ = pieces[i]
                k = i % 4
                dma(stf[k][:, 0:cw_], src[r0:r0 + 128, c0:c0 + cw_], "stf%d" % k, writes=["stf%d" % k])

            for i in range(min(3, len(pieces))):
                p0_load(i)
            for i, (src, dst, r0, c0, cw_) in enumerate(pieces):
                if i + 3 < len(pieces):
                    p0_load(i + 3)
                k = i % 4
                if i % 2 == 0:
                    op("act", lambda e, k=k, cw_=cw_: e.copy(out=stb[k][:, 0:cw_], in_=stf[k][:, 0:cw_]),
                       reads=["stf%d" % k], writes=["stb%d" % k])
                else:
                    op("dve", lambda e, k=k, cw_=cw_: e.tensor_copy(out=stb[k][:, 0:cw_], in_=stf[k][:, 0:cw_]),
                       reads=["stf%d" % k], writes=["stb%d" % k])
                dma(dst[r0:r0 + 128, c0:c0 + cw_], stb[k][:, 0:cw_], "stb%d" % k, reads=["stb%d" % k])
            sc.barrier()
            sc.new_epoch()
            if _STOP == 1:
                return nc

        def rstd_of(ss_ap, n, tmp_ap, out_ap, res):
            op("act", lambda e: e.activation(out=tmp_ap, in_=ss_ap, func=AF.Sqrt, scale=1.0 / n, bias=EPS),
               reads=[res], writes=[res])
            op("dve", lambda e: e.reciprocal(out=out_ap, in_=tmp_ap), reads=[res], writes=[res])

        def norm_transpose(x_ap, xres, g_t, gres, np_, dst_fn, dstres, W, part=None, kbuf=None):
            if part == "tr":
                k = kbuf
            else:
                k = W["i"] % 2
                W["i"] += 1
            junk, nb, st = W["junk"], W["nb"][k], W["st"][k]
            jr, nr, sr = "junk", "nb%d" % k, "nst%d" % k
            if part != "tr":
                op("act", lambda e: e.activation(out=junk[0:np_, :], in_=x_ap, func=AF.Square, accum_out=st[0:np_, 0:1]),
                   reads=[xres], writes=[jr, sr])
                rstd_of(st[0:np_, 0:1], D, st[0:np_, 1:2], st[0:np_, 2:3], sr)
                op("dve", lambda e: e.scalar_tensor_tensor(out=nb[0:np_, :], in0=x_ap, scalar=st[0:np_, 2:3],
                                                           in1=g_t[0:np_, :], op0=ALU.mult, op1=ALU.mult),
                   reads=[xres, sr, gres], writes=[nr])
                if part == "stats":
                    return k
            for hh in range(2):
                h = sc.half()
                for q in range(4):
                    kc = hh * 4 + q
                    op("pe", lambda e, kc=kc, h=h, q=q: e.transpose(psb_half(h)[:, q, 0:np_], nb[0:np_, kc * 128:(kc + 1) * 128],
                                                                    identb[0:np_, 0:np_]),
                       reads=[nr, "identb"], writes=["psb%d" % h], sig=(q == 3))
                en = sc.pick("act", "dve")
                if en == "act":
                    op("act", lambda e, h=h, hh=hh: e.copy(out=dst_fn(hh * 4), in_=psb_half(h)[:, :, 0:np_]),
                       reads=["psb%d" % h], writes=[dstres])
                else:
                    op("dve", lambda e, h=h, hh=hh: e.tensor_copy(out=dst_fn(hh * 4), in_=psb_half(h)[:, :, 0:np_]),
                       reads=["psb%d" % h], writes=[dstres])

        def transpose_cols(src_ap, srcres, np_, nch, dst_fn, dstres):
            for k0 in range(0, nch, 4):
                n = min(4, nch - k0)
                h = sc.half()
                for q in range(n):
                    kc = k0 + q
                    op("pe", lambda e, kc=kc, h=h, q=q: e.transpose(psb_half(h)[:, q, 0:np_], src_ap[:, kc * 128:(kc + 1) * 128],
                                                                    identb[0:np_, 0:np_]),
                       reads=[srcres, "identb"], writes=["psb%d" % h], sig=(q == n - 1))
                en = sc.pick("act", "dve")
                if en == "act":
                    op("act", lambda e, h=h, k0=k0, n=n: e.copy(out=dst_fn(k0, n), in_=psb_half(h)[:, 0:n, 0:np_]),
                       reads=["psb%d" % h], writes=[dstres])
                else:
                    op("dve", lambda e, h=h, k0=k0, n=n: e.tensor_copy(out=dst_fn(k0, n), in_=psb_half(h)[:, 0:n, 0:np_]),
                       reads=["psb%d" % h], writes=[dstres])

        def proj(nT, nTres, np_, w_t, wres, c0):
            bk = sc.bank()
            for kc in range(8):
                op("pe", lambda e, kc=kc, bk=bk: e.matmul(PS[0:np_, bk, :], lhsT=nT[:, kc, 0:np_], rhs=w_t[:, kc, c0:c0 + 512],
                                                          start=(kc == 0), stop=(kc == 7)),
                   reads=[nTres] + list(wres), writes=["ps%d" % bk], sig=(kc == 7))
            return bk

        def rope_a(bk, np_, tab_ap, tabres, out_ap, outres, W):
            k = W["ia"] % 2
            W["ia"] += 1
            ra, rr_ = W["ra"][k], "ra%d" % k
            psv = PS[0:np_, bk, :].rearrange("p (a d) -> p a d", a=8)
            ov = out_ap.rearrange("p (a d) -> p a d", a=8)
            cb = tab_ap[:, 0:8].unsqueeze(1).to_broadcast([np_, 8, 8])
            sbc = tab_ap[:, 8:16].unsqueeze(1).to_broadcast([np_, 8, 8])
            rav = ra[0:np_].rearrange("p (s a d) -> p s a d", s=4, a=8)
            pr = "ps%d" % bk
            op("act", lambda e: e.copy(out=ov[:, :, 16:64], in_=psv[:, :, 16:64]), reads=[pr], writes=[outres])
            op("dve", lambda e: e.tensor_tensor(out=rav[:, 0], in0=psv[:, :, 0:8], in1=cb, op=ALU.mult),
               reads=[pr, tabres], writes=[rr_ + "a"])
            op("dve", lambda e: e.tensor_tensor(out=rav[:, 1], in0=psv[:, :, 8:16], in1=sbc, op=ALU.mult),
               reads=[pr, tabres], writes=[rr_ + "b"])
            op("dve", lambda e: e.tensor_tensor(out=rav[:, 2], in0=psv[:, :, 8:16], in1=cb, op=ALU.mult),
               reads=[pr, tabres], writes=[rr_ + "c"])
            op("dve", lambda e: e.tensor_tensor(out=rav[:, 3], in0=psv[:, :, 0:8], in1=sbc, op=ALU.mult),
               reads=[pr, tabres], writes=[rr_ + "d"])
            op("dve", lambda e: e.tensor_tensor(out=ov[:, :, 0:8], in0=rav[:, 0], in1=rav[:, 1], op=ALU.subtract),
               reads=[rr_ + "a", rr_ + "b"], writes=[outres])
            op("dve", lambda e: e.tensor_tensor(out=ov[:, :, 8:16], in0=rav[:, 2], in1=rav[:, 3], op=ALU.add),
               reads=[rr_ + "c", rr_ + "d"], writes=[outres])

        def rope_b(ps_ap, pr, np_, nh, gc_ap, gs_ap, gres, out_ap, outres, W, er=None):
            k = W["ib"] % 2
            W["ib"] += 1
            sq, xr, t1, t2, st = W["sq"][k], W["xr"][k], W["t1"][k], W["t2"][k], W["bst"][k]
            n_ = nh * 64
            kk = "rb%d" % k
            op("act", lambda e: e.activation(out=sq[0:np_, 0:n_], in_=ps_ap, func=AF.Square), reads=[pr], writes=[kk + "sq"])
            op("dve", lambda e: e.tensor_reduce(out=st[0:np_, 0:nh], in_=sq[0:np_, 0:n_].rearrange("p (a d) -> p a d", a=nh),
                                                axis=AX.X, op=ALU.add), reads=[kk + "sq"], writes=[kk + "st"])
            rstd_of(st[0:np_, 0:nh], 64, st[0:np_, 8:8 + nh], st[0:np_, 16:16 + nh], kk + "st")
            xrv = xr[0:np_, 0:n_].rearrange("p (a d) -> p a d", a=nh)
            t1v = t1[0:np_, 0:n_].rearrange("p (a d) -> p a d", a=nh)
            op("dve", lambda e: e.tensor_tensor(out=xrv, in0=ps_ap.rearrange("p (a d) -> p a d", a=nh),
                                                in1=st[0:np_, 16:16 + nh].unsqueeze(2).to_broadcast([np_, nh, 64]), op=ALU.mult),
               reads=[pr, kk + "st"], writes=[kk + "xr"])
            op("pool", lambda e: e.tensor_tensor(out=t1v, in0=xrv, in1=gc_ap.unsqueeze(1).to_broadcast([np_, nh, 64]), op=ALU.mult),
               reads=[kk + "xr", gres], writes=[kk + "t1"])
            x5 = xr[0:np_, 0:n_].rearrange("p (a h q e) -> p a h q e", a=nh, h=2, q=2)
            t5 = t2[0:np_, 0:n_].rearrange("p (a h q e) -> p a h q e", a=nh, h=2, q=2)
            g4 = gs_ap.rearrange("p (h q e) -> p h q e", h=2, q=2)
            for qo, qi in ((0, 1), (1, 0)):
                op("dve", lambda e, qo=qo, qi=qi: e.tensor_tensor(
                    out=t5[:, :, :, qo, :], in0=x5[:, :, :, qi, :],
                    in1=g4[:, :, qo, :].unsqueeze(1).to_broadcast([np_, nh, 2, 16]), op=ALU.mult),
                   reads=[kk + "xr", gres], writes=[kk + "t2"])
            if er is None:
                i0, i1 = t1v, t2[0:np_, 0:n_].rearrange("p (a d) -> p a d", a=nh)
            else:
                i0 = t1[0:np_, 0:n_].rearrange("p (e r d) -> p e r d", e=er[0], r=er[1])
                i1 = t2[0:np_, 0:n_].rearrange("p (e r d) -> p e r d", e=er[0], r=er[1])
            op("pool", lambda e: e.tensor_tensor(out=out_ap, in0=i0, in1=i1, op=ALU.add),
               reads=[kk + "t1", kk + "t2"], writes=[outres])

        def make_tabs(tb_ap, tbres, np_, W):
            k = W["it"] % 2
            W["it"] += 1
            gt = W["gt"][k]
            r = "gt%d" % k
            for i_, (g_, gr) in enumerate(((gq_t, "gq"), (gk_t, "gk"))):
                op("pool", lambda e, i_=i_, g_=g_: e.tensor_tensor(out=gt[0:np_, 2 * i_, :], in0=tb_ap[:, 0:64], in1=g_[0:np_, :],
                                                                  op=ALU.mult), reads=[tbres, gr], writes=[r])
            for i_, (g_, gr) in enumerate(((gqsw, "gqsw"), (gksw, "gksw"))):
                op("pool", lambda e, i_=i_, g_=g_: e.tensor_tensor(out=gt[0:np_, 2 * i_ + 1, :], in0=tb_ap[:, 64:128],
                                                                  in1=g_[0:np_, :], op=ALU.mult), reads=[tbres, gr], writes=[r])
            return gt, r

        for u in range(NU):
            with ExitStack() as ph:
                def pb(name, shape, dt=F32):
                    return ph.enter_context(nc.sbuf_tensor(_uniq(name), list(shape), dt))
                wkv = pb("wkv", [128, 8, 2560], BF16)
                for kc in range(8):
                    dma(wkv[:, kc, 0:2048], Wb_in[kc * 128:(kc + 1) * 128, 1024:3072], "wkv", writes=["wkv%da" % kc])
                    dma(wkv[:, kc, 2048:2560], Wb_in[kc * 128:(kc + 1) * 128, 4096:4608], "wkv", writes=["wkv%db" % kc])
                WKV = tuple("wkv%d%s" % (kc, ab) for kc in range(8) for ab in "ab")
                WQ = tuple("wq%d%s" % (kc, ab) for kc in range(8) for ab in "ab")
                gmix_t = pb("gmix_t", [128, D])
                dma(gmix_t[:], g_mix.partition_broadcast(128), "c_gmix", writes=["gmix"])
                xb_ = [pb("xb%d" % i, [128, D]) for i in range(3)]
                tA = [pb("tA%d" % i, [128, 16]) for i in range(3)]
                tB = [pb("tB%d" % i, [128, 128]) for i in range(3)]
                W = dict(i=0, ia=0, ib=0, it=0, junk=pb("junk", [128, D], BF16),
                         nb=[pb("nb%d" % i, [128, D], BF16) for i in range(2)],
                         st=[pb("nst%d" % i, [128, 4]) for i in range(2)],
                         ra=[pb("ra%d" % i, [128, 256]) for i in range(2)],
                         sq=[pb("sq%d" % i, [128, 512]) for i in range(2)],
                         xr=[pb("xr%d" % i, [128, 512]) for i in range(2)],
                         t1=[pb("t1%d" % i, [128, 512]) for i in range(2)],
                         t2=[pb("t2%d" % i, [128, 512]) for i in range(2)],
                         bst=[pb("bst%d" % i, [128, 24]) for i in range(2)],
                         gt=[pb("gt%d" % i, [128, 4, 64]) for i in range(2)])
                nT = [pb("nT%d" % i, [128, 8, 128], BF16) for i in range(2)]
                ktm = [pb("ktm%d" % i, [128, D], BF16) for i in range(2)]
                vtm = [pb("vtm%d" % i, [128, D], BF16) for i in range(2)]
                bkt = [pb("bkt%d" % i, [128, 256], BF16) for i in range(2)]
                bvt = [pb("bvt%d" % i, [128, 256], BF16) for i in range(2)]
                kst = [pb("kst%d" % i, [128, 8, 128], BF16) for i in range(2)]
                bks = [pb("bks%d" % i, [128, 2, 128], BF16) for i in range(2)]
                nsb = S // 128

                def load1a(s):
                    k = s % 3
                    dma(xb_[k][:], xs[u, s * 128:(s + 1) * 128, :], "xb%d" % k, writes=["xb%d" % k])
                    dma(tA[k][:], tabAk[s * 128:(s + 1) * 128, :], "tA%d" % k, writes=["tA%d" % k])
                    dma(tB[k][:], tabBk[s * 128:(s + 1) * 128, :], "tB%d" % k, writes=["tB%d" % k])

                nkb = {}

                def norm1a(s, part):
                    k3, k2 = s % 3, s % 2
                    r = norm_transpose(xb_[k3][:], "xb%d" % k3, gmix_t, "gmix", 128,
                                       lambda k0, k2=k2: nT[k2][:, k0:k0 + 4, :], "nT%d" % k2, W, part=part, kbuf=nkb.get(s))
                    if part == "stats":
                        nkb[s] = r

                def fin1a(s):
                    k2, t0 = s % 2, s * 128
                    transpose_cols(ktm[k2][:], "ktm%d" % k2, 128, 8,
                                   lambda k0, n, k2=k2: kst[k2][:, k0:k0 + n, :], "kst%d" % k2)
                    dma(KTA[:, :, t0:t0 + 128].rearrange("h p t -> p h t"), kst[k2][:], "kst%d" % k2, reads=["kst%d" % k2])
                    transpose_cols(bkt[k2][:], "bkt%d" % k2, 128, 2,
                                   lambda k0, n, k2=k2: bks[k2][:, k0:k0 + n, :], "bks%d" % k2)
                    dma(KTB[:, :, t0:t0 + 128].rearrange("c p t -> p c t"), bks[k2][:], "bks%d" % k2, reads=["bks%d" % k2])

                load1a(0)
                if nsb > 1:
                    load1a(1)
                norm1a(0, "stats")
                norm1a(0, "tr")
                for s in range(nsb):
                    if s + 2 < nsb:
                        load1a(s + 2)
                    if s + 1 < nsb:
                        norm1a(s + 1, "stats")
                    k3, k2 = s % 3, s % 2
                    t0 = s * 128
                    gt, gtr = make_tabs(tB[k3][:], "tB%d" % k3, 128, W)
                    for cc in range(2):
                        bk = proj(nT[k2], "nT%d" % k2, 128, wkv, WKV, cc * 512)
                        rope_a(bk, 128, tA[k3][:], "tA%d" % k3, ktm[k2][:, cc * 512:(cc + 1) * 512], "ktm%d" % k2, W)
                    bk = proj(nT[k2], "nT%d" % k2, 128, wkv, WKV, 2048)
                    rope_b(PS[:, bk, 0:256], "ps%d" % bk, 128, 4, gt[:, 2, :], gt[:, 3, :], gtr,
                           bkt[k2][:].rearrange("p (a d) -> p a d", a=4), "bkt%d" % k2, W)
                    op("act", lambda e, bk=bk: e.copy(out=bvt[k2][:], in_=PS[:, bk, 256:512]), reads=["ps%d" % bk],
                       writes=["bvt%d" % k2])
                    dma(VB[:, t0:t0 + 128, :].rearrange("g t e -> t g e"), bvt[k2][:].rearrange("p (g e) -> p g e", g=4),
                        "bvt%d" % k2, reads=["bvt%d" % k2])
                    for cc in range(2):
                        bk = proj(nT[k2], "nT%d" % k2, 128, wkv, WKV, 1024 + cc * 512)
                        en = sc.pick("act", "dve")
                        if en == "act":
                            op("act", lambda e, bk=bk, cc=cc: e.copy(out=vtm[k2][:, cc * 512:(cc + 1) * 512], in_=PS[:, bk, :]),
                               reads=["ps%d" % bk], writes=["vtm%d" % k2])
                        else:
                            op("dve", lambda e, bk=bk, cc=cc: e.tensor_copy(out=vtm[k2][:, cc * 512:(cc + 1) * 512], in_=PS[:, bk, :]),
                               reads=["ps%d" % bk], writes=["vtm%d" % k2])
                    dma(VA[:, t0:t0 + 128, :].rearrange("h t e -> t h e"), vtm[k2][:].rearrange("p (h e) -> p h e", h=8),
                        "vtm%d" % k2, reads=["vtm%d" % k2])
                    if s + 1 < nsb:
                        norm1a(s + 1, "tr")
                    if s >= 1:
                        fin1a(s - 1)
                fin1a(nsb - 1)
                sc.barrier()
                if _STOP == 2:
                    return nc

            with ExitStack() as ph:
                def pb(name, shape, dt=F32):
                    return ph.enter_context(nc.sbuf_tensor(_uniq(name), list(shape), dt))
                wq = pb("wq", [128, 8, 2048], BF16)
                for kc in range(8):
                    dma(wq[:, kc, 0:1024], Wb_in[kc * 128:(kc + 1) * 128, 0:1024], "wq", writes=["wq%da" % kc])
                    dma(wq[:, kc, 1024:2048], Wb_in[kc * 128:(kc + 1) * 128, 3072:4096], "wq", writes=["wq%db" % kc])
                WKV = tuple("wkv%d%s" % (kc, ab) for kc in range(8) for ab in "ab")
                WQ = tuple("wq%d%s" % (kc, ab) for kc in range(8) for ab in "ab")
                gmix_t = pb("gmix_t", [128, D])
                dma(gmix_t[:], g_mix.partition_broadcast(128), "c_gmix", writes=["gmix"])
                xb_ = [pb("xb%d" % i, [128, D]) for i in range(3)]
                tA = [pb("tA%d" % i, [128, 16]) for i in range(3)]
                tB = [pb("tB%d" % i, [128, 128]) for i in range(3)]
                W = dict(i=0, ia=0, ib=0, it=0, junk=pb("junk", [128, D], BF16),
                         nb=[pb("nb%d" % i, [128, D], BF16) for i in range(2)],
                         st=[pb("nst%d" % i, [128, 4]) for i in range(2)],
                         ra=[pb("ra%d" % i, [128, 256]) for i in range(2)],
                         sq=[pb("sq%d" % i, [128, 512]) for i in range(2)],
                         xr=[pb("xr%d" % i, [128, 512]) for i in range(2)],
                         t1=[pb("t1%d" % i, [128, 512]) for i in range(2)],
                         t2=[pb("t2%d" % i, [128, 512]) for i in range(2)],
                         bst=[pb("bst%d" % i, [128, 24]) for i in range(2)],
                         gt=[pb("gt%d" % i, [128, 4, 64]) for i in range(2)])
                nT = [pb("nT%d" % i, [128, 8, 128], BF16) for i in range(2)]
                qtm = [pb("qtm%d" % i, [128, D], BF16) for i in range(2)]
                bqt = [pb("bqt%d" % i, [128, D], BF16) for i in range(2)]
                qst = [pb("qst%d" % i, [128, 8, 128], BF16) for i in range(2)]
                qbs = [pb("qbs%d" % i, [128, 8, 128], BF16) for i in range(2)]
                blocks = [(s * 128, 128) for s in range(NSUBQ)] + [(QN, HW)]

                def load1b(i):
                    t0, np_ = blocks[i]
                    k = i % 3
                    dma(xb_[k][0:np_, :], xq[u, t0:t0 + np_, :], "xb%d" % k, writes=["xb%d" % k])
                    dma(tA[k][0:np_, :], tabAq[u, t0:t0 + np_, :], "tA%d" % k, writes=["tA%d" % k])
                    dma(tB[k][0:np_, :], tabBq[u, t0:t0 + np_, :], "tB%d" % k, writes=["tB%d" % k])

                nkb = {}

                def norm1b(i, part):
                    t0, np_ = blocks[i]
                    k3, k2 = i % 3, i % 2
                    r = norm_transpose(xb_[k3][0:np_, :], "xb%d" % k3, gmix_t, "gmix", np_,
                                       lambda k0, k2=k2, np_=np_: nT[k2][:, k0:k0 + 4, 0:np_], "nT%d" % k2, W, part=part, kbuf=nkb.get(i))
                    if part == "stats":
                        nkb[i] = r
                    else:
                        dma(NT[:, :, t0:t0 + np_].rearrange("k p t -> p k t"), nT[k2][:, :, 0:np_], "nT%d" % k2, reads=["nT%d" % k2])

                def fin1b(i):
                    t0, np_ = blocks[i]
                    k2 = i % 2
                    transpose_cols(qtm[k2][0:np_, :], "qtm%d" % k2, np_, 8,
                                   lambda k0, n, k2=k2, np_=np_: qst[k2][:, k0:k0 + n, 0:np_], "qst%d" % k2)
                    dma(QTA[:, :, t0:t0 + np_].rearrange("h p t -> p h t"), qst[k2][:, :, 0:np_], "qst%d" % k2, reads=["qst%d" % k2])
                    transpose_cols(bqt[k2][0:np_, :], "bqt%d" % k2, np_, 8,
                                   lambda k0, n, k2=k2, np_=np_: qbs[k2][:, k0:k0 + n, 0:np_], "qbs%d" % k2)
                    for c in range(2):
                        if np_ == 128:
                            col = (t0 // 128) * 512
                            dma(QTB[c, :, col:col + 512].rearrange("p (r t) -> p r t", r=4), qbs[k2][:, 4 * c:4 * c + 4, :],
                                "qbs%d" % k2, reads=["qbs%d" % k2])
                        else:
                            col = 4 * QN
                            dma(QTB[c, :, col:col + 4 * HW].rearrange("p (r t) -> p r t", r=4), qbs[k2][:, 4 * c:4 * c + 4, 0:HW],
                                "qbs%d" % k2, reads=["qbs%d" % k2])

                load1b(0)
                load1b(1)
                norm1b(0, "stats")
                norm1b(0, "tr")
                for i, (t0, np_) in enumerate(blocks):
                    if i + 2 < len(blocks):
                        load1b(i + 2)
                    if i + 1 < len(blocks):
                        norm1b(i + 1, "stats")
                    k3, k2 = i % 3, i % 2
                    gt, gtr = make_tabs(tB[k3][0:np_, :], "tB%d" % k3, np_, W)
                    for cc in range(2):
                        bk = proj(nT[k2], "nT%d" % k2, np_, wq, WQ, cc * 512)
                        rope_a(bk, np_, tA[k3][0:np_, :], "tA%d" % k3, qtm[k2][0:np_, cc * 512:(cc + 1) * 512], "qtm%d" % k2, W)
                    for c in range(2):
                        bk = proj(nT[k2], "nT%d" % k2, np_, wq, WQ, 1024 + c * 512)
                        ov = bqt[k2][0:np_, c * 512:(c + 1) * 512].rearrange("p (r e d) -> p e r d", r=4, e=2)
                        rope_b(PS[0:np_, bk, :], "ps%d" % bk, np_, 8, gt[0:np_, 0, :], gt[0:np_, 1, :], gtr,
                               ov, "bqt%d" % k2, W, er=(2, 4))
                    if i + 1 < len(blocks):
                        norm1b(i + 1, "tr")
                    if i >= 1:
                        fin1b(i - 1)
                fin1b(len(blocks) - 1)
                sc.barrier()
                if _STOP == 3:
                    return nc

            with ExitStack() as ph:
                def pb(name, shape, dt=F32):
                    return ph.enter_context(nc.sbuf_tensor(_uniq(name), list(shape), dt))
                KT = [pb("KT%d" % i, [128, S], BF16) for i in range(2)]
                VtA = [pb("VtA%d" % i, [128, NKB, 128], BF16) for i in range(2)]
                VtB = [pb("VtB%d" % i, [128, 2, NKB, 64], BF16) for i in range(1)]
                QtA = [pb("QtA%d" % i, [128, QT], BF16) for i in range(2)]
                QtB = [pb("QtB%d" % i, [128, 4 * QT], BF16) for i in range(1)]
                NPT = 4
                PT = [pb("PT%d" % i, [128, 2, 512], BF16) for i in range(NPT)]
                P2 = [pb("P2%d" % i, [128, 2, 512], BF16) for i in range(2)]
                P4 = [pb("P4%d" % i, [128, 2, 512], BF16) for i in range(2)]
                osb = [pb("osb%d" % i, [128, 2, 512]) for i in range(2)]
                rl = [pb("rl%d" % i, [128, 2, 512]) for i in range(2)]
                pa = [pb("pa%d" % i, [128, 2, 512]) for i in range(2)]
                oab = [pb("oab%d" % i, [128, 2, 512], BF16) for i in range(2)]

                def BK(i):
                    return PS[:, i, :] if i < 6 else PSB[:, i - 6, :].bitcast(F32)

                jobs = [("A", h) for h in range(8)] + [("B", c) for c in range(2)]

                def load_job(ji, part):
                    kind, idx = jobs[ji]
                    k = ji % 2
                    if kind == "A":
                        if part == 1:
                            return
                        dma(KT[k][:], KTA[idx], "KT%d" % k, writes=["KT%d" % k])
                        dma(VtA[k][:], VA[idx].rearrange("(kb p) e -> p kb e", p=128), "VtA%d" % k, writes=["VtA%d" % k])
                        dma(QtA[k][:], QTA[idx], "QtA%d" % k, writes=["QtA%d" % k])
                    elif part == 0:
                        dma(KT[k][:], KTB[idx], "KT%d" % k, writes=["KT%d" % k])
                    else:
                        for m in range(2):
                            dma(VtB[0][:, m, :, :], VB[2 * idx + m].rearrange("(kb p) e -> p kb e", p=128), "VtB0", writes=["VtB0"])
                        dma(QtB[0][:], QTB[idx], "QtB0", writes=["QtB0"])

                cnt = dict(step=0, post=0)

                def attention(kt, ktr, q_ap, qr, v_fn, vr, dv, col0, Wd, pre=None):
                    base = cnt["step"]
                    cnt["step"] += NKB

                    def scores(kb):
                        b_ = (base + kb) % 2
                        for m in range(2):
                            op("pe", lambda e, m=m: e.matmul(
                                PS[:, 2 * b_ + m, 0:Wd], lhsT=kt[64 * m:64 * m + 64, kb * 128:(kb + 1) * 128],
                                rhs=q_ap[64 * m:64 * m + 64, col0:col0 + Wd], start=True, stop=True),
                               reads=[ktr, qr], writes=["S%d" % b_], sig=(m == 1))

                    def expo(kb):
                        b_ = (base + kb) % 2
                        p_ = (base + kb) % NPT
                        op("act", lambda e: e.activation(out=PT[p_][:, :, 0:Wd], in_=PS[:, 2 * b_:2 * b_ + 2, 0:Wd],
                                                         func=AF.Exp, scale=0.125),
                           reads=["S%d" % b_], writes=["PT%d" % p_])

                    def psum4(kb):
                        if kb % 2 == 1:
                            a_, b2 = (base + kb - 1) % NPT, (base + kb) % NPT
                            j2 = (kb // 2) % 2
                            op("dve", lambda e: e.tensor_tensor(out=P2[j2][:, :, 0:Wd], in0=PT[a_][:, :, 0:Wd], in1=PT[b2][:, :, 0:Wd],
                                                                op=ALU.add), reads=["PT%d" % a_, "PT%d" % b2], writes=["P2%d" % j2])
                        if kb % 4 == 3 and kb != NKB - 1:
                            j4 = (kb // 4) % 2
                            op("dve", lambda e: e.tensor_tensor(out=P4[j4][:, :, 0:Wd], in0=P2[0][:, :, 0:Wd], in1=P2[1][:, :, 0:Wd],
                                                                op=ALU.add), reads=["P20", "P21"], writes=["P4%d" % j4])

                    def av(kb):
                        b_ = (base + kb) % NPT
                        for m in range(2):
                            op("pe", lambda e, m=m: e.matmul(BK(4 + m)[0:dv, 0:Wd], lhsT=v_fn(m, kb), rhs=PT[b_][:, m, 0:Wd],
                                                             start=(kb == 0), stop=(kb == NKB - 1)),
                               reads=["PT%d" % b_, vr], writes=["Ob%d" % (4 + m)], sig=(m == 1))
                        if kb % 4 == 0 and kb >= 4:
                            g_ = kb // 4 - 1
                            for m in range(2):
                                op("pe", lambda e, m=m: e.matmul(BK(6 + m)[0:dv, 0:Wd], lhsT=onesb[:, 0:dv], rhs=P4[g_ % 2][:, m, 0:Wd],
                                                                 start=(g_ == 0), stop=False),
                                   reads=["P4%d" % (g_ % 2), "onesb"], writes=["Ob%d" % (6 + m)], sig=(m == 1))
                        if kb == NKB - 1:
                            for j2 in range(2):
                                for m in range(2):
                                    op("pe", lambda e, m=m, j2=j2: e.matmul(BK(6 + m)[0:dv, 0:Wd], lhsT=onesb[:, 0:dv], rhs=P2[j2][:, m, 0:Wd],
                                                                            start=(NKB == 4 and j2 == 0), stop=(j2 == 1)),
                                       reads=["P2%d" % j2, "onesb"], writes=["Ob%d" % (6 + m)], sig=(m == 1))

                    scores(0)
                    for kb in range(NKB):
                        if kb + 1 < NKB:
                            scores(kb + 1)
                        expo(kb)
                        psum4(kb)
                        if pre and (kb == 1 or (kb >= 3 and kb % 3 == 0)):
                            pre.pop(0)()
                        if kb >= 1:
                            av(kb - 1)
                    while pre:
                        pre.pop(0)()
                    av(NKB - 1)

                def post(kind, idx, dv, Wd, tok0):
                    k = cnt["post"] % 2
                    cnt["post"] += 1
                    st = []

                    def evac_o():
                        for m in range(2):
                            op("dve", lambda e, m=m: e.tensor_copy(out=osb[k][0:dv, m, 0:Wd], in_=BK(4 + m)[0:dv, 0:Wd]),
                               reads=["Ob%d" % (4 + m)], writes=["osb%d_%d" % (k, m)])

                    def evac_l():
                        for m in range(2):
                            op("dve", lambda e, m=m: e.tensor_copy(out=rl[k][0:dv, m, 0:Wd], in_=BK(6 + m)[0:dv, 0:Wd]),
                               reads=["Ob%d" % (6 + m)], writes=["rl%d_%d" % (k, m)])
                    st.append(evac_o)
                    st.append(evac_l)
                    for m in range(2):
                        for c0 in range(0, Wd, 128):
                            c1 = min(Wd, c0 + 128)
                            st.append(lambda m=m, c0=c0, c1=c1: op(
                                "dve", lambda e: e.reciprocal(out=rl[k][0:dv, m, c0:c1], in_=rl[k][0:dv, m, c0:c1]),
                                reads=["rl%d_%d" % (k, m)], writes=["rl%d_%d" % (k, m)]))
                    RL = ["rl%d_0" % k, "rl%d_1" % k]

                    def fin():
                        if kind == "A":
                            op("pool", lambda e: e.tensor_tensor(out=pa[k][:, :, 0:Wd], in0=osb[k][:, :, 0:Wd], in1=rl[k][:, :, 0:Wd],
                                                                 op=ALU.mult), reads=["osb%d_0" % k, "osb%d_1" % k] + RL, writes=["pa%d" % k])
                            op("dve", lambda e: e.scalar_tensor_tensor(out=oab[k][:, 0, 0:Wd], in0=pa[k][:, 1, 0:Wd], scalar=lams[:, 6:7],
                                                                       in1=pa[k][:, 0, 0:Wd], op0=ALU.mult, op1=ALU.add),
                               reads=["pa%d" % k, "lams"], writes=["oab%d" % k])
                            dma(OT[idx, :, tok0:tok0 + Wd], oab[k][:, 0, 0:Wd], "oab%d" % k, reads=["oab%d" % k])
                        else:
                            op("pool", lambda e: e.tensor_tensor(out=oab[k][0:64, :, 0:Wd], in0=osb[k][0:64, :, 0:Wd],
                                                                 in1=rl[k][0:64, :, 0:Wd], op=ALU.mult),
                               reads=["osb%d_0" % k, "osb%d_1" % k] + RL, writes=["oab%d" % k])
                            OTf = OT.rearrange("c p t -> (c p) t")
                            wq_ = Wd // 4
                            for m in range(2):
                                g = 2 * idx + m
                                dma(OTf[1024 + g * 256:1024 + (g + 1) * 256, tok0:tok0 + wq_].rearrange("(r d) q -> d r q", r=4),
                                    oab[k][0:64, m, 0:Wd].rearrange("d (r q) -> d r q", r=4), "oab%d" % k, reads=["oab%d" % k])
                    st.append(fin)
                    return st

                pend = [None]
                load_job(0, 0)
                for ji, (kind, idx) in enumerate(jobs):
                    if ji + 1 < len(jobs):
                        load_job(ji + 1, 0)
                        if kind == "A":
                            load_job(ji + 1, 1)
                    if kind == "B" and ji > 0 and jobs[ji - 1][0] == "B":
                        load_job(ji, 1)
                    k = ji % 2
                    if kind == "A":
                        qbl = [(b * 512, 512, b * 512) for b in range(QN // 512)] + [(QN, HW, QN)]
                        for (col0, Wd, tok0) in qbl:
                            attention(KT[k], "KT%d" % k, QtA[k], "QtA%d" % k, lambda m, kb, k=k: VtA[k][:, kb, :], "VtA%d" % k, 128, col0, Wd,
                                      pre=pend[0])
                            pend[0] = post("A", idx, 128, Wd, tok0)
                    else:
                        qbl = [(s_ * 512, 512, s_ * 128) for s_ in range(NSUBQ)] + [(4 * QN, 4 * HW, QN)]
                        for (col0, Wd, tok0) in qbl:
                            attention(KT[k], "KT%d" % k, QtB[0], "QtB0", lambda m, kb: VtB[0][:, m, kb, :], "VtB0", 64, col0, Wd,
                                      pre=pend[0])
                            pend[0] = post("B", idx, 64, Wd, tok0)
                while pend[0]:
                    pend[0].pop(0)()
                sc.barrier()
                if _STOP == 4:
                    return nc

            with ExitStack() as ph3:
                n2T = ph3.enter_context(nc.sbuf_tensor(_uniq("n2T"), [128, 8, QT], BF16))
                with ExitStack() as ph:
                    def pb(name, shape, dt=F32):
                        return ph.enter_context(nc.sbuf_tensor(_uniq(name), list(shape), dt))
                    wa = pb("wa", [128, 8, D], BF16)
                    wb = pb("wb", [128, 8, D], BF16)
                    wo = pb("wo", [128, 8, D], BF16)
                    wg = pb("wg", [128, 8, 2048], BF16)
                    for kc in range(8):
                        r = slice(kc * 128, (kc + 1) * 128)
                        dma(wa[:, kc, :], Wb_a[r, :], "wa", writes=["wa%d" % kc])
                        dma(wb[:, kc, :], Wb_b[r, :], "wb", writes=["wb%d" % kc])
                        dma(wg[:, kc, :], Wb_in[r, 4608:6656], "wg", writes=["wg%d" % kc])
                        dma(wo[:, kc, :], Wb_out[r, :], "wo", writes=["wo%d" % kc])
                    gffn_t = pb("gffn_t", [128, D])
                    dma(gffn_t[:], g_ffn.partition_broadcast(128), "c_gffn", writes=["gffn"])
                    TB = 256
                    ota = [pb("ota%d" % i, [128, 8, TB], BF16) for i in range(2)]
                    otb = [pb("otb%d" % i, [128, 8, TB], BF16) for i in range(2)]
                    ntg = [pb("ntg%d" % i, [128, 8, TB], BF16) for i in range(2)]
                    xh = [pb("xh%d" % i, [128, 2, D]) for i in range(2)]
                    mT = pb("mT", [128, 8, TB], BF16)
                    sq8 = pb("sq8", [128, 8, TB], BF16)
                    rt8 = pb("rt8", [128, 8, TB])
                    sg = [pb("sg%d" % i, [128, 2, 256]) for i in range(2)]
                    tt = [pb("tt%d" % i, [128, 2, 256]) for i in range(2)]
                    W = dict(i=0, junk=pb("junk", [128, D], BF16),
                             nb=[pb("nb%d" % i, [128, D], BF16) for i in range(2)],
                             st=[pb("nst%d" % i, [128, 4]) for i in range(2)])
                    blocks = [(b * TB, TB, TB // 128, 128) for b in range(QN // TB)] + [(QN, HW, 1, HW)]

                    def load3a(i):
                        t0, Wd, nsub, wsub = blocks[i]
                        k = i % 2
                        dma(ota[k][:, :, 0:Wd], OT[0:8, :, t0:t0 + Wd].rearrange("k p t -> p k t"), "ota%d" % k, writes=["ota%d" % k])
                        dma(otb[k][:, :, 0:Wd], OT[8:16, :, t0:t0 + Wd].rearrange("k p t -> p k t"), "otb%d" % k, writes=["otb%d" % k])
                        dma(ntg[k][:, :, 0:Wd], NT[:, :, t0:t0 + Wd].rearrange("k p t -> p k t"), "ntg%d" % k, writes=["ntg%d" % k])
                        dma(xh[k][0:wsub, 0:nsub, :], xq[u, t0:t0 + Wd, :].rearrange("(j p) d -> p j d", p=wsub), "xh%d" % k,
                            writes=["xh%d" % k])

                    def prep3a(i):
                        t0, Wd, nsub, wsub = blocks[i]
                        k = i % 2
                        op("pool", lambda e: e.tensor_tensor(out=sq8[:, :, 0:Wd], in0=ota[k][:, :, 0:Wd], in1=ota[k][:, :, 0:Wd], op=ALU.mult),
                           reads=["ota%d" % k], writes=["sq8"])
                        bks_ = []
                        for hp in range(4):
                            bk = sc.bank()
                            bks_.append(bk)
                            for z in range(2):
                                op("pe", lambda e, bk=bk, z=z, hp=hp: e.matmul(PS[:, bk, z * 256:z * 256 + Wd], lhsT=onesb[:], rhs=sq8[:, 2 * hp + z, 0:Wd],
                                                                              start=True, stop=True, skip_group_check=True),
                                   reads=["sq8", "onesb"], writes=["ps%d" % bk], sig=(z == 1))
                        for hp in range(4):
                            bk = bks_[hp]
                            op("act", lambda e, bk=bk, hp=hp: e.activation(out=rt8[:, 2 * hp:2 * hp + 2, 0:Wd],
                                                                          in_=PS[:, bk, :].rearrange("p (a b) -> p a b", a=2)[:, :, 0:Wd],
                                                                          func=AF.Sqrt, scale=1.0 / 128, bias=EPS),
                               reads=["ps%d" % bk], writes=["rt8_%d" % hp])
                            op("dve", lambda e, hp=hp: e.reciprocal(out=rt8[:, 2 * hp:2 * hp + 2, 0:Wd], in_=rt8[:, 2 * hp:2 * hp + 2, 0:Wd]),
                               reads=["rt8_%d" % hp], writes=["rt8_%d" % hp])
                            for z in range(2):
                                h = 2 * hp + z
                                op("dve", lambda e, h=h: e.scalar_tensor_tensor(out=ota[k][:, h, 0:Wd], in0=ota[k][:, h, 0:Wd], scalar=gdc[:, 0:1],
                                                                               in1=rt8[:, h, 0:Wd], op0=ALU.mult, op1=ALU.mult),
                                   reads=["ota%d" % k, "rt8_%d" % hp, "gdc"], writes=["ota%d" % k])

                    load3a(0)
                    prep3a(0)
                    for i, (t0, Wd, nsub, wsub) in enumerate(blocks):
                        if i + 1 < len(blocks):
                            load3a(i + 1)
                        k = i % 2
                        for oc in range(8):
                            q2 = oc % 2
                            bX, bY = sc.bank(), sc.bank()
                            for gi, (wt, wr, c0, rt, rr_, bk, off) in enumerate(((wa, "wa", 0, ota[k], "ota%d" % k, bX, 0), (wb, "wb", 0, otb[k], "otb%d" % k, bX, 256),
                                                                                 (wg, "wg", 0, ntg[k], "ntg%d" % k, bY, 0), (wg, "wg", 1024, ntg[k], "ntg%d" % k, bY, 256))):
                                for kc in range(8):
                                    op("pe", lambda e, kc=kc, bk=bk, wt=wt, c0=c0, rt=rt, off=off: e.matmul(
                                        PS[:, bk, off:off + Wd], lhsT=wt[:, kc, c0 + oc * 128:c0 + (oc + 1) * 128], rhs=rt[:, kc, 0:Wd],
                                        start=(kc == 0), stop=(kc == 7), skip_group_check=True),
                                       reads=[wr + "%d" % kc, rr_], writes=["ps%d" % bk], sig=(kc == 7))
                            op("act", lambda e, bY=bY, q2=q2: e.activation(out=sg[q2][:, :, 0:Wd], in_=PS[:, bY, :].rearrange("p (a b) -> p a b", a=2)[:, :, 0:Wd],
                                                                          func=AF.Sigmoid), reads=["ps%d" % bY], writes=["sg%d" % q2])
                            op("dve", lambda e, bX=bX, q2=q2: e.tensor_tensor(out=tt[q2][:, :, 0:Wd], in0=PS[:, bX, :].rearrange("p (a b) -> p a b", a=2)[:, :, 0:Wd],
                                                                             in1=sg[q2][:, :, 0:Wd], op=ALU.mult),
                               reads=["ps%d" % bX, "sg%d" % q2], writes=["tt%d" % q2])
                            op("pool", lambda e, q2=q2, oc=oc: e.tensor_tensor(out=mT[:, oc, 0:Wd], in0=tt[q2][:, 0, 0:Wd], in1=tt[q2][:, 1, 0:Wd], op=ALU.add),
                               reads=["tt%d" % q2], writes=["mT"])
                        if i + 1 < len(blocks):
                            prep3a(i + 1)
                        for j in range(nsub):
                            for hf_ in range(2):
                                bk = sc.bank()
                                for kc in range(8):
                                    op("pe", lambda e, kc=kc, bk=bk, j=j, hf_=hf_: e.matmul(
                                        PS[0:wsub, bk, :], lhsT=mT[:, kc, j * wsub:(j + 1) * wsub], rhs=wo[:, kc, hf_ * 512:(hf_ + 1) * 512],
                                        start=(kc == 0), stop=(kc == 7)), reads=["mT", "wo%d" % kc], writes=["ps%d" % bk], sig=(kc == 7))
                                op("dve", lambda e, bk=bk, j=j, hf_=hf_: e.tensor_tensor(
                                    out=xh[k][0:wsub, j, hf_ * 512:(hf_ + 1) * 512], in0=PS[0:wsub, bk, :],
                                    in1=xh[k][0:wsub, j, hf_ * 512:(hf_ + 1) * 512], op=ALU.add),
                                   reads=["ps%d" % bk, "xh%d" % k], writes=["xh%d" % k])
                            if wsub == 128:
                                dma(H2[t0 + j * 128:t0 + (j + 1) * 128, :], xh[k][:, j, :], "xh%d" % k, reads=["xh%d" % k])
                            tt0 = t0 + j * wsub
                            norm_transpose(xh[k][0:wsub, j, :], "xh%d" % k, gffn_t, "gffn", wsub,
                                           lambda k0, tt0=tt0, wsub=wsub: n2T[:, k0:k0 + 4, tt0:tt0 + wsub], "n2T", W)
                    sc.barrier()
                    if _STOP == 5:
                        return nc

                with ExitStack() as ph:
                    def pb(name, shape, dt=F32):
                        return ph.enter_context(nc.sbuf_tensor(_uniq(name), list(shape), dt))
                    wup = [pb("wup%d" % i, [128, 8, 2, 128], BF16) for i in range(2)]
                    U = [pb("U%d" % i, [128, QN + 2]) for i in range(2)]
                    cv = [pb("cv%d" % i, [128, QN]) for i in range(2)]
                    G = pb("G", [128, QN])
                    t2c = [pb("t2c%d" % i, [128, QN]) for i in range(2)]
                    actb = [pb("actb%d" % i, [128, QN], BF16) for i in range(2)]

                    def load3b(fc):
                        k = fc % 2
                        for br in range(2):
                            c0 = br * DFF + fc * 128
                            dma(wup[k][:, :, br, :], Wb_up[:, c0:c0 + 128].rearrange("(k p) f -> p k f", p=128), "wup%d" % k,
                                writes=["wup%d" % k])

                    load3b(0)
                    for fc in range(NFC):
                        if fc + 1 < NFC:
                            load3b(fc + 1)
                        k = fc % 2
                        for br in range(2):
                            cch = fc + NFC * br
                            for blk in range(QN // 512):
                                bk = sc.bank()
                                for kc in range(8):
                                    op("pe", lambda e, kc=kc, bk=bk, blk=blk, br=br: e.matmul(
                                        PS[:, bk, :], lhsT=wup[k][:, kc, br, :], rhs=n2T[:, kc, blk * 512:(blk + 1) * 512],
                                        start=(kc == 0), stop=(kc == 7)), reads=["wup%d" % k, "n2T"], writes=["ps%d" % bk], sig=(kc == 7))
                                op("act", lambda e, bk=bk, blk=blk, br=br: e.copy(out=U[br][:, 1 + blk * 512:1 + (blk + 1) * 512], in_=PS[:, bk, :]),
                                   reads=["ps%d" % bk], writes=["U%d" % br])
                            bk = sc.bank()
                            for kc in range(8):
                                op("pe", lambda e, kc=kc, bk=bk, br=br: e.matmul(
                                    PS[:, bk, 0:HW], lhsT=wup[k][:, kc, br, :], rhs=n2T[:, kc, QN:QN + HW],
                                    start=(kc == 0), stop=(kc == 7)), reads=["wup%d" % k, "n2T"], writes=["ps%d" % bk], sig=(kc == 7))
                            op("dve", lambda e, bk=bk, br=br: e.tensor_scalar(out=U[br][:, 0:1], in0=PS[:, bk, 0:1], scalar1=hm_t[:, 2 * u:2 * u + 1],
                                                                             scalar2=None, op0=ALU.mult),
                               reads=["ps%d" % bk, "hm"], writes=["U%d" % br])
                            op("dve", lambda e, bk=bk, br=br: e.tensor_scalar(out=U[br][:, QN + 1:QN + 2], in0=PS[:, bk, 1:2],
                                                                             scalar1=hm_t[:, 2 * u + 1:2 * u + 2], scalar2=None, op0=ALU.mult),
                               reads=["ps%d" % bk, "hm"], writes=["U%d" % br])
                            op("act", lambda e, br=br, cch=cch: e.activation(out=cv[br][:], in_=U[br][:, 0:QN], func=AF.Copy,
                                                                            scale=cwt[:, 0, cch:cch + 1]),
                               reads=["U%d" % br, "cwt"], writes=["cv%d" % br])
                            op("act", lambda e, br=br, cch=cch: e.activation(out=t2c[br][:], in_=U[br][:, 2:QN + 2], func=AF.Copy,
                                                                            scale=cwt[:, 2, cch:cch + 1]),
                               reads=["U%d" % br, "cwt"], writes=["t2c%d" % br])
                            op("dve", lambda e, br=br, cch=cch: e.scalar_tensor_tensor(
                                out=cv[br][:], in0=U[br][:, 1:1 + QN], scalar=cwt[:, 1, cch:cch + 1], in1=cv[br][:],
                                op0=ALU.mult, op1=ALU.add), reads=["U%d" % br, "cwt", "cv%d" % br], writes=["cv%d" % br])
                            op("pool", lambda e, br=br: e.tensor_tensor(out=cv[br][:], in0=cv[br][:], in1=t2c[br][:], op=ALU.add),
                               reads=["cv%d" % br, "t2c%d" % br], writes=["cv%d" % br])
                        op("act", lambda e, fc=fc: e.activation(out=G[:], in_=cv[0][:], func=AF.Gelu_apprx_tanh, bias=cwt[:, 3, fc:fc + 1]),
                           reads=["cv0", "cwt"], writes=["G"])
                        op("dve", lambda e, fc=fc: e.scalar_tensor_tensor(out=actb[k][:], in0=cv[1][:], scalar=cwt[:, 3, NFC + fc:NFC + fc + 1],
                                                                          in1=G[:], op0=ALU.add, op1=ALU.mult),
                           reads=["cv1", "cwt", "G"], writes=["actb%d" % k])
                        dma(ACTS[fc], actb[k][:], "actb%d" % k, reads=["actb%d" % k])
                    sc.barrier()
                    if _STOP == 6:
                        return nc

            with ExitStack() as ph:
                def pb(name, shape, dt=F32):
                    return ph.enter_context(nc.sbuf_tensor(_uniq(name), list(shape), dt))
                wd = pb("wd", [128, NFC, D], BF16)
                wpg = pb("wpg", [128, 8, D], BF16)
                wpl = pb("wpl", [128, 2, D], BF16)
                for f in range(NFC):
                    dma(wd[:, f, :], Wb_down[f * 128:(f + 1) * 128, :], "wd", writes=["wd%d" % f])
                for kc in range(8):
                    dma(wpg[:, kc, :], Wb_pg[kc * 128:(kc + 1) * 128, :], "wpg", writes=["wpg%d" % kc])
                for kc in range(2):
                    dma(wpl[:, kc, :], Wb_ple[kc * 128:(kc + 1) * 128, :], "wpl", writes=["wpl%d" % kc])
                gple_t = pb("gple_t", [128, D])
                gfin_t = pb("gfin_t", [128, D])
                dma(gple_t[:], g_ple.partition_broadcast(128), "c_gple", writes=["gple"])
                dma(gfin_t[:], g_final.partition_broadcast(128), "c_gfin", writes=["gfin"])
                at = [pb("at%d" % i, [128, NFC, 128], BF16) for i in range(2)]
                h2t = [pb("h2t%d" % i, [128, D]) for i in range(2)]
                pt = [pb("pt%d" % i, [128, 256]) for i in range(2)]
                ptb = pb("ptb", [128, 256], BF16)
                pT = pb("pT", [128, 2, 128], BF16)
                n3T = pb("n3T", [128, 8, 128], BF16)
                sgt = pb("sgt", [128, D])
                t4 = pb("t4", [128, D])
                ot = [pb("ot%d" % i, [128, D]) for i in range(2)]
                fst = pb("fst", [128, 4])
                W = dict(i=0, junk=pb("junk", [128, D], BF16),
                         nb=[pb("nb%d" % i, [128, D], BF16) for i in range(2)],
                         st=[pb("nst%d" % i, [128, 4]) for i in range(2)])

                def load3c(s):
                    k = s % 2
                    dma(at[k][:], ACTS[:, :, s * 128:(s + 1) * 128].rearrange("f p t -> p f t"), "at%d" % k, writes=["at%d" % k])
                    dma(h2t[k][:], H2[s * 128:(s + 1) * 128, :], "h2t%d" % k, writes=["h2t%d" % k])
                    dma(pt[k][:], pq[u, s * 128:(s + 1) * 128, :], "pt%d" % k, writes=["pt%d" % k])

                load3c(0)
                for s in range(NSUBQ):
                    if s + 1 < NSUBQ:
                        load3c(s + 1)
                    k = s % 2
                    for hf_ in range(2):
                        bk = sc.bank()
                        for f in range(NFC):
                            op("pe", lambda e, f=f, bk=bk, hf_=hf_: e.matmul(PS[:, bk, :], lhsT=at[k][:, f, :], rhs=wd[:, f, hf_ * 512:(hf_ + 1) * 512],
                                                                            start=(f == 0), stop=(f == NFC - 1)),
                               reads=["at%d" % k, "wd%d" % f], writes=["ps%d" % bk], sig=(f == NFC - 1))
                        op("dve", lambda e, bk=bk, hf_=hf_: e.tensor_tensor(out=h2t[k][:, hf_ * 512:(hf_ + 1) * 512], in0=PS[:, bk, :],
                                                                           in1=h2t[k][:, hf_ * 512:(hf_ + 1) * 512], op=ALU.add),
                           reads=["ps%d" % bk, "h2t%d" % k], writes=["h2t%d" % k])
                    norm_transpose(h2t[k][:], "h2t%d" % k, gple_t, "gple", 128, lambda k0: n3T[:, k0:k0 + 4, :], "n3T", W)
                    op("act", lambda e: e.copy(out=ptb[:], in_=pt[k][:]), reads=["pt%d" % k], writes=["ptb"])
                    transpose_cols(ptb[:], "ptb", 128, 2, lambda k0, n: pT[:, k0:k0 + n, :], "pT")
                    gb_, pb_ = [], []
                    for hf_ in range(2):
                        bk = sc.bank()
                        gb_.append(bk)
                        for kc in range(8):
                            op("pe", lambda e, kc=kc, bk=bk, hf_=hf_: e.matmul(PS[:, bk, :], lhsT=n3T[:, kc, :], rhs=wpg[:, kc, hf_ * 512:(hf_ + 1) * 512],
                                                                              start=(kc == 0), stop=(kc == 7)),
                               reads=["n3T", "wpg%d" % kc], writes=["ps%d" % bk], sig=(kc == 7))
                        bk = sc.bank()
                        pb_.append(bk)
                        for kc in range(2):
                            op("pe", lambda e, kc=kc, bk=bk, hf_=hf_: e.matmul(PS[:, bk, :], lhsT=pT[:, kc, :], rhs=wpl[:, kc, hf_ * 512:(hf_ + 1) * 512],
                                                                              start=(kc == 0), stop=(kc == 1)),
                               reads=["pT", "wpl%d" % kc], writes=["ps%d" % bk], sig=(kc == 1))
                    for hf_ in range(2):
                        cs = slice(hf_ * 512, (hf_ + 1) * 512)
                        op("act", lambda e, hf_=hf_, cs=cs: e.activation(out=sgt[:, cs], in_=PS[:, gb_[hf_], :], func=AF.Sigmoid),
                           reads=["ps%d" % gb_[hf_]], writes=["sgt%d" % hf_])
                        op("dve", lambda e, hf_=hf_, cs=cs: e.tensor_tensor(out=t4[:, cs], in0=PS[:, pb_[hf_], :], in1=sgt[:, cs], op=ALU.mult),
                           reads=["ps%d" % pb_[hf_], "sgt%d" % hf_], writes=["t4"])
                    op("pool", lambda e: e.tensor_tensor(out=t4[:], in0=t4[:], in1=h2t[k][:], op=ALU.add),
                       reads=["t4", "h2t%d" % k], writes=["t4"])
                    op("act", lambda e: e.activation(out=W["junk"][:], in_=t4[:], func=AF.Square, accum_out=fst[:, 0:1]),
                       reads=["t4"], writes=["junk", "fst"])
                    rstd_of(fst[:, 0:1], D, fst[:, 1:2], fst[:, 2:3], "fst")
                    op("dve", lambda e: e.scalar_tensor_tensor(out=ot[k][:], in0=t4[:], scalar=fst[:, 2:3], in1=gfin_t[:],
                                                               op0=ALU.mult, op1=ALU.mult),
                       reads=["t4", "fst", "gfin"], writes=["ot%d" % k])
                    dma(yout[u, s * 128:(s + 1) * 128, :], ot[k][:], "ot%d" % k, reads=["ot%d" % k])
                sc.barrier()
                sc.new_epoch()
                if _STOP == 7:
                    return nc
    return nc


def _rope_tab(pos, dim, theta):
    inv = np.power(np.float32(theta), -(np.arange(0, dim, 2, dtype=np.float32) / np.float32(dim))).astype(np.float32)
    ang = (pos.astype(np.float32)[:, None] * inv[None, :]).astype(np.float32)
    return np.cos(ang.astype(np.float64)).astype(np.float32), np.sin(ang.astype(np.float64)).astype(np.float32)


def _tables(pos):
    ca, sa = _rope_tab(pos, 16, 500000.0)
    cr, sr = _rope_tab(pos // 64, 32, 10000.0)
    cc, sc_ = _rope_tab(pos % 64, 32, 10000.0)
    tA = np.concatenate([ca, sa], axis=1).astype(np.float32)
    tB = np.concatenate([cr, cr, cc, cc, -sr, sr, -sc_, sc_], axis=1).astype(np.float32)
    return np.ascontiguousarray(tA), np.ascontiguousarray(tB)


def run_layer(seqs_x, seqs_p, wts, S, QN, assign):
    ncores = len(assign)
    NU = len(assign[0])
    nc = build(S, QN, NU)
    tAk, tBk = _tables(np.arange(S))
    in_maps = []
    for c in range(ncores):
        xs = np.stack([seqs_x[sq] for sq, _ in assign[c]])
        rows, hm = [], []
        for sq, part in assign[c]:
            t0 = part * QN
            hb, ha = t0 - 1, t0 + QN
            r = np.concatenate([np.arange(t0, t0 + QN), [max(hb, 0), min(ha, S - 1)], np.full(HW - 2, max(hb, 0))])
            rows.append(r.astype(np.int64))
            hm += [1.0 if hb >= 0 else 0.0, 1.0 if ha < S else 0.0]
        xq = np.stack([seqs_x[sq][r] for (sq, _), r in zip(assign[c], rows)])
        pq = np.stack([seqs_p[sq][r[:QN]] for (sq, _), r in zip(assign[c], rows)])
        tq = [_tables(r) for r in rows]
        m = dict(xs=xs, xq=xq, pq=pq, tabAk=tAk, tabBk=tBk,
                 tabAq=np.stack([t[0] for t in tq]), tabBq=np.stack([t[1] for t in tq]),
                 hmask=np.asarray([hm], dtype=np.float32), ident=np.eye(128, dtype=np.float32))
        m.update(wts)
        in_maps.append({k: np.ascontiguousarray(v, dtype=np.float32) for k, v in m.items()})
    res = run_bass_kernel_spmd(nc, in_maps, core_ids=list(range(ncores)))
    nseq = seqs_x.shape[0]
    _LAST["res"] = res.results
    out = np.zeros((nseq, S, D), dtype=np.float32)
    for c in range(ncores):
        y = res.results[c]["y"]
        for ui, (sq, part) in enumerate(assign[c]):
            out[sq, part * QN:(part + 1) * QN] = y[ui]
    return out


def _weights(kw):
    w = {}
    for k in ("g_mix", "lambda_q1", "lambda_k1", "lambda_q2", "lambda_k2", "g_diff", "g_qn", "g_kn", "g_ffn", "conv_b", "g_ple"):
        w[k] = np.asarray(kw[k]).reshape(1, -1)
    w["g_final"] = np.asarray(kw["g_final"]).reshape(1, -1)
    for k in ("w_in", "w_a", "w_b", "w_out", "w_up", "conv_w", "w_down", "w_ple", "w_ple_gate"):
        w[k] = np.asarray(kw[k])[0]
    return w


def kernel(**inputs):
    xp = np.asarray(inputs["x_prompt"])
    xsm = np.asarray(inputs["x_sample"])
    pp = np.asarray(inputs["p_prompt"])[0]
    psm = np.asarray(inputs["p_sample"])[0]
    seqs_x = np.concatenate([xp, xsm], axis=0)
    seqs_p = np.concatenate([pp, psm], axis=0)
    S = seqs_x.shape[1]
    QN = S // 4
    assign = [[(3 * (c // 4) + i, c % 4) for i in range(3)] for c in range(8)]
    out = run_layer(seqs_x, seqs_p, _weights(inputs), S, QN, assign)
    nb = xp.shape[0]
    return (np.ascontiguousarray(out[:nb]), np.ascontiguousarray(out[nb:]))
```

```python
import math
from contextlib import ExitStack

import numpy as np
import concourse.bass as bass
import concourse.mybir as mybir
from concourse.bass_utils import run_bass_kernel_spmd

F32 = mybir.dt.float32
BF16 = mybir.dt.bfloat16
AF = mybir.ActivationFunctionType
ALU = mybir.AluOpType
AX = mybir.AxisListType

D = 1024
DFF = 2816
NFC = DFF // 128
EPS = 1e-6
LAM_INIT = 0.8 - 0.6 * math.exp(-0.3 * 0)
HW = 32
_STOP = 99
_NOPOOL = False
_SKIP = set()
_DUMP = set()
_LAST = {}


_UID = [0]


def _uniq(name):
    _UID[0] += 1
    return "%s_%d" % (name, _UID[0])


class Sched:
    def __init__(self, nc, stack):
        self.nc = nc
        self.stack = stack
        self.eng = {"pe": nc.tensor, "act": nc.scalar, "dve": nc.vector, "pool": nc.gpsimd, "sp": nc.sync}
        self.sems = {}
        self.count = {}
        self.seen = {e: {} for e in self.eng}
        self.lastw = {}
        self.readers = {}
        self.rr = 0
        self.dma_slot = {}
        self.epoch = 0
        self.nbank = 0
        self.nhalf = 0

    def _sem(self, name):
        if name not in self.sems:
            self.sems[name] = self.stack.enter_context(self.nc.semaphore(name))
            self.count[name] = 0
        return self.sems[name]

    def new_epoch(self):
        self.epoch += 1

    def op(self, e, fn, reads=(), writes=(), sig=True, dma=None):
        if e == "pool" and _NOPOOL:
            e = "dve"
        xr = [r for r in reads if r.startswith(("ps", "S", "Ob"))]
        if xr:
            writes = list(writes) + xr
        own = "e%d_%s" % (self.epoch, e)
        waits = {}

        def need(tok):
            if tok is None:
                return
            s, v = tok
            if s == own and e == "pe":
                return
            if self.seen[e].get(s, 0) >= v:
                return
            waits[s] = max(waits.get(s, 0), v)

        for r in reads:
            need(self.lastw.get(r))
        for w in writes:
            need(self.lastw.get(w))
            for s, v in self.readers.get(w, {}).items():
                need((s, v))
        eng = self.eng[e]
        for s, v in waits.items():
            assert v <= self.count[s], ("wait on unsignalled", e, s, v, self.count[s])
            eng.wait_ge(self.sems[s], v)
            self.seen[e][s] = v
        ins = fn(eng)
        if dma is not None:
            s = "d%d" % self.dma_slot.setdefault(dma, len(self.dma_slot))
            h = self._sem(s)
            assert self.count[s] < 30000
            self.count[s] += 16
            ins.then_inc(h, 16)
            tok = (s, self.count[s])
        else:
            h = self._sem(own)
            if sig:
                self.count[own] += 1
                ins.then_inc(h, 1)
                tok = (own, self.count[own])
            else:
                tok = (own, self.count[own] + 1)
        for r in reads:
            d = self.readers.setdefault(r, {})
            d[tok[0]] = max(d.get(tok[0], 0), tok[1])
        for w in writes:
            self.lastw[w] = tok
            self.readers[w] = {}
        return tok

    def barrier(self):
        for e, eng in self.eng.items():
            for s, c in self.count.items():
                if c > 0 and self.seen[e].get(s, 0) < c and not (s == "e%d_%s" % (self.epoch, e)):
                    eng.wait_ge(self.sems[s], c)
                    self.seen[e][s] = c
        self.lastw = {}
        self.readers = {}
        self.dma_slot = {}

    def bank(self):
        b = self.nbank % 6
        self.nbank += 1
        return b

    def half(self):
        h = self.nhalf % 2
        self.nhalf += 1
        return h

    def pick(self, *engs):
        self.rr += 1
        return engs[self.rr % len(engs)]


def build(S, QN, NU):
    NKB = S // 128
    QT = QN + HW
    NSUBQ = QN // 128
    nc = bass.Bass("TRN2", target_bir_lowering=False)

    def din(name, shape):
        return nc.dram_tensor(name, list(shape), F32, kind="ExternalInput").ap()

    def dscr(name, shape, dt=BF16):
        return nc.dram_tensor(name, list(shape), dt, kind=("ExternalOutput" if name in _DUMP else "Internal")).ap()

    xs = din("xs", [NU, S, D])
    xq = din("xq", [NU, QT, D])
    pq = din("pq", [NU, QN, 256])
    tabAk = din("tabAk", [S, 16])
    tabBk = din("tabBk", [S, 128])
    tabAq = din("tabAq", [NU, QT, 16])
    tabBq = din("tabBq", [NU, QT, 128])
    hmask = din("hmask", [1, 2 * NU])
    ident = din("ident", [128, 128])
    g_mix = din("g_mix", [1, D])
    w_in = din("w_in", [D, 6656])
    lq1 = din("lambda_q1", [1, 64])
    lk1 = din("lambda_k1", [1, 64])
    lq2 = din("lambda_q2", [1, 64])
    lk2 = din("lambda_k2", [1, 64])
    g_diff = din("g_diff", [1, 128])
    w_a = din("w_a", [D, D])
    g_qn = din("g_qn", [1, 64])
    g_kn = din("g_kn", [1, 64])
    w_b = din("w_b", [D, D])
    w_out = din("w_out", [D, D])
    g_ffn = din("g_ffn", [1, D])
    w_up = din("w_up", [D, 2 * DFF])
    conv_w = din("conv_w", [3, 2 * DFF])
    conv_b = din("conv_b", [1, 2 * DFF])
    w_down = din("w_down", [DFF, D])
    w_ple = din("w_ple", [256, D])
    g_ple = din("g_ple", [1, D])
    w_pg = din("w_ple_gate", [D, D])
    g_final = din("g_final", [1, D])
    yout = nc.dram_tensor("y", [NU, QN, D], F32, kind="ExternalOutput").ap()

    Wb_in = dscr("Wb_in", [D, 6656])
    Wb_a = dscr("Wb_a", [D, D])
    Wb_b = dscr("Wb_b", [D, D])
    Wb_out = dscr("Wb_out", [D, D])
    Wb_up = dscr("Wb_up", [D, 2 * DFF])
    Wb_down = dscr("Wb_down", [DFF, D])
    Wb_ple = dscr("Wb_ple", [256, D])
    Wb_pg = dscr("Wb_pg", [D, D])
    KTA = dscr("KTA", [8, 128, S])
    KTB = dscr("KTB", [2, 128, S])
    VA = dscr("VA", [8, S, 128])
    VB = dscr("VB", [4, S, 64])
    QTA = dscr("QTA", [8, 128, QT])
    QTB = dscr("QTB", [2, 128, 4 * QT])
    NT = dscr("NT", [8, 128, QT])
    OT = dscr("OT", [16, 128, QT])
    H2 = dscr("H2", [QN, D], F32)
    ACTS = dscr("ACTS", [NFC, 128, QN])

    with ExitStack() as stack:
        sc = Sched(nc, stack)
        op = sc.op

        def sb(name, shape, dt=F32):
            return stack.enter_context(nc.sbuf_tensor(_uniq(name), list(shape), dt))

        PS = stack.enter_context(nc.psum_tensor("PS", [128, 6, 512], F32))
        PSB = stack.enter_context(nc.psum_tensor("PSB", [128, 2, 1024], BF16))

        def psb_half(h):
            return PSB[:, h, 0:512].rearrange("p (a b) -> p a b", a=4)

        def dma(out, in_, key, reads=(), writes=(), q="sp"):
            return op(q, lambda e: e.dma_start(out=out, in_=in_), reads=reads, writes=writes, dma=key)

        identf = sb("identf", [128, 128])
        identb = sb("identb", [128, 128], BF16)
        gq_t = sb("gq_t", [128, 64])
        gk_t = sb("gk_t", [128, 64])
        gqsw = sb("gqsw", [128, 64])
        gksw = sb("gksw", [128, 64])
        gd_t = sb("gd_t", [128, 128])
        lamv = sb("lamv", [128, 4, 64])
        lams = sb("lams", [128, 8])
        hm_t = sb("hm_t", [128, 2 * NU])
        cwt = sb("cwt", [128, 4, 2 * NFC])
        onesb = sb("onesb", [128, 128], BF16)
        gdc = sb("gdc", [128, 1])

        dma(identf[:], ident[:, :], "c0", writes=["identf"])
        op("dve", lambda e: e.tensor_copy(out=identb[:], in_=identf[:]), reads=["identf"], writes=["identb"])
        for t, src, nm in ((gq_t, g_qn, "gq"), (gk_t, g_kn, "gk"),
                           (gd_t, g_diff, "gd"), (hm_t, hmask, "hm")):
            dma(t[:], src.partition_broadcast(128), "c_" + nm, writes=[nm])
        for i, src in enumerate((lq1, lk1, lq2, lk2)):
            dma(lamv[:, i, :], src.partition_broadcast(128), "c_l%d" % i, writes=["lamv%d" % i])
        for gt, gs, nm in ((gq_t, gqsw, "gq"), (gk_t, gksw, "gk")):
            for a, b in ((0, 16), (16, 0), (32, 48), (48, 32)):
                op("dve", lambda e, gs=gs, gt=gt, a=a, b=b: e.tensor_copy(out=gs[:, a:a + 16], in_=gt[:, b:b + 16]),
                   reads=[nm], writes=[nm + "sw"])
        op("dve", lambda e: e.tensor_tensor(out=lamv[:, 0, :], in0=lamv[:, 0, :], in1=lamv[:, 1, :], op=ALU.mult),
           reads=["lamv0", "lamv1"], writes=["lamv0"])
        op("dve", lambda e: e.tensor_tensor(out=lamv[:, 2, :], in0=lamv[:, 2, :], in1=lamv[:, 3, :], op=ALU.mult),
           reads=["lamv2", "lamv3"], writes=["lamv2"])
        op("dve", lambda e: e.tensor_reduce(out=lams[:, 0:1], in_=lamv[:, 0, :], axis=AX.X, op=ALU.add),
           reads=["lamv0"], writes=["lams"])
        op("dve", lambda e: e.tensor_reduce(out=lams[:, 1:2], in_=lamv[:, 2, :], axis=AX.X, op=ALU.add),
           reads=["lamv2", "lams"], writes=["lams"])
        op("act", lambda e: e.activation(out=lams[:, 2:4], in_=lams[:, 0:2], func=AF.Exp), reads=["lams"], writes=["lams"])
        op("dve", lambda e: e.tensor_tensor(out=lams[:, 4:5], in0=lams[:, 2:3], in1=lams[:, 3:4], op=ALU.subtract),
           reads=["lams"], writes=["lams"])
        op("dve", lambda e: e.tensor_scalar(out=lams[:, 5:6], in0=lams[:, 4:5], scalar1=float(LAM_INIT), scalar2=None,
                                            op0=ALU.add), reads=["lams"], writes=["lams"])
        lam_ap = lams[:, 5:6]
        op("dve", lambda e: e.tensor_scalar(out=lams[:, 6:7], in0=lams[:, 5:6], scalar1=-1.0, scalar2=None, op0=ALU.mult),
           reads=["lams"], writes=["lams"])
        op("pool", lambda e: e.memset(onesb[:], 1.0), writes=["onesb"])
        dma(gdc[:], g_diff.rearrange("o d -> d o"), "c_gdc", writes=["gdc"])
        op("dve", lambda e: e.tensor_scalar(out=gdc[:], in0=gdc[:], scalar1=float(1.0 - LAM_INIT), scalar2=None, op0=ALU.mult),
           reads=["gdc"], writes=["gdc"])
        op("dve", lambda e: e.tensor_scalar(out=gd_t[:], in0=gd_t[:], scalar1=float(1.0 - LAM_INIT), scalar2=None,
                                            op0=ALU.mult), reads=["gd"], writes=["gd"])

        with ExitStack() as ph:
            def psb_(name, shape, dt=F32):
                return ph.enter_context(nc.sbuf_tensor(_uniq(name), list(shape), dt))
            stf = [psb_("stf%d" % i, [128, 2048]) for i in range(4)]
            stb = [psb_("stb%d" % i, [128, 2048], BF16) for i in range(4)]
            cwb = psb_("cwb", [2 * NFC, 4, 128])
            for j in range(3):
                dma(cwb[:, j, :], conv_w[j].rearrange("(c p) -> c p", p=128), "cw%d" % j, writes=["cwb"])
            dma(cwb[:, 3, :], conv_b[0].rearrange("(c p) -> c p", p=128), "cw3", writes=["cwb"])
            for j in range(4):
                bk = sc.bank()
                op("pe", lambda e, j=j, bk=bk: e.transpose(PS[:, bk, 0:2 * NFC], cwb[:, j, :], identf[0:2 * NFC, 0:2 * NFC]),
                   reads=["cwb", "identf"], writes=["ps%d" % bk])
                op("dve", lambda e, j=j, bk=bk: e.tensor_copy(out=cwt[:, j, :], in_=PS[:, bk, 0:2 * NFC]),
                   reads=["ps%d" % bk], writes=["cwt"])
            pieces = []
            for src, dst in ((w_in, Wb_in), (w_a, Wb_a), (w_b, Wb_b), (w_out, Wb_out), (w_up, Wb_up),
                             (w_down, Wb_down), (w_ple, Wb_ple), (w_pg, Wb_pg)):
                R, C = src.shape
                for r0 in range(0, R, 128):
                    for c0 in range(0, C, 2048):
                        pieces.append((src, dst, r0, c0, min(2048, C - c0)))

            def p0_load(i):
                src, dst, r0, c0, cw_ = pieces[i]
                k = i % 4
                dma(stf[k][:, 0:cw_], src[r0:r0 + 128, c0:c0 + cw_], "stf%d" % k, writes=["stf%d" % k])

            for i in range(min(3, len(pieces))):
                p0_load(i)
            for i, (src, dst, r0, c0, cw_) in enumerate(pieces):
                if i + 3 < len(pieces):
                    p0_load(i + 3)
                k = i % 4
                if i % 2 == 0:
                    op("act", lambda e, k=k, cw_=cw_: e.copy(out=stb[k][:, 0:cw_], in_=stf[k][:, 0:cw_]),
                       reads=["stf%d" % k], writes=["stb%d" % k])
                else:
                    op("dve", lambda e, k=k, cw_=cw_: e.tensor_copy(out=stb[k][:, 0:cw_], in_=stf[k][:, 0:cw_]),
                       reads=["stf%d" % k], writes=["stb%d" % k])
                dma(dst[r0:r0 + 128, c0:c0 + cw_], stb[k][:, 0:cw_], "stb%d" % k, reads=["stb%d" % k])
            sc.barrier()
            sc.new_epoch()
            if _STOP == 1:
                return nc

        def rstd_of(ss_ap, n, tmp_ap, out_ap, res):
            op("act", lambda e: e.activation(out=tmp_ap, in_=ss_ap, func=AF.Sqrt, scale=1.0 / n, bias=EPS),
               reads=[res], writes=[res])
            op("dve", lambda e: e.reciprocal(out=out_ap, in_=tmp_ap), reads=[res], writes=[res])

        def norm_transpose(x_ap, xres, g_t, gres, np_, dst_fn, dstres, W, part=None, kbuf=None):
            if part == "tr":
                k = kbuf
            else:
                k = W["i"] % 2
                W["i"] += 1
            junk, nb, st = W["junk"], W["nb"][k], W["st"][k]
            jr, nr, sr = "junk", "nb%d" % k, "nst%d" % k
            if part != "tr":
                op("act", lambda e: e.activation(out=junk[0:np_, :], in_=x_ap, func=AF.Square, accum_out=st[0:np_, 0:1]),
                   reads=[xres], writes=[jr, sr])
                rstd_of(st[0:np_, 0:1], D, st[0:np_, 1:2], st[0:np_, 2:3], sr)
                op("dve", lambda e: e.scalar_tensor_tensor(out=nb[0:np_, :], in0=x_ap, scalar=st[0:np_, 2:3],
                                                           in1=g_t[0:np_, :], op0=ALU.mult, op1=ALU.mult),
                   reads=[xres, sr, gres], writes=[nr])
                if part == "stats":
                    return k
            for hh in range(2):
                h = sc.half()
                for q in range(4):
                    kc = hh * 4 + q
                    op("pe", lambda e, kc=kc, h=h, q=q: e.transpose(psb_half(h)[:, q, 0:np_], nb[0:np_, kc * 128:(kc + 1) * 128],
                                                                    identb[0:np_, 0:np_]),
                       reads=[nr, "identb"], writes=["psb%d" % h], sig=(q == 3))
                en = sc.pick("act", "dve")
                if en == "act":
                    op("act", lambda e, h=h, hh=hh: e.copy(out=dst_fn(hh * 4), in_=psb_half(h)[:, :, 0:np_]),
                       reads=["psb%d" % h], writes=[dstres])
                else:
                    op("dve", lambda e, h=h, hh=hh: e.tensor_copy(out=dst_fn(hh * 4), in_=psb_half(h)[:, :, 0:np_]),
                       reads=["psb%d" % h], writes=[dstres])

        def transpose_cols(src_ap, srcres, np_, nch, dst_fn, dstres):
            for k0 in range(0, nch, 4):
                n = min(4, nch - k0)
                h = sc.half()
                for q in range(n):
                    kc = k0 + q
                    op("pe", lambda e, kc=kc, h=h, q=q: e.transpose(psb_half(h)[:, q, 0:np_], src_ap[:, kc * 128:(kc + 1) * 128],
                                                                    identb[0:np_, 0:np_]),
                       reads=[srcres, "identb"], writes=["psb%d" % h], sig=(q == n - 1))
                en = sc.pick("act", "dve")
                if en == "act":
                    op("act", lambda e, h=h, k0=k0, n=n: e.copy(out=dst_fn(k0, n), in_=psb_half(h)[:, 0:n, 0:np_]),
                       reads=["psb%d" % h], writes=[dstres])
                else:
                    op("dve", lambda e, h=h, k0=k0, n=n: e.tensor_copy(out=dst_fn(k0, n), in_=psb_half(h)[:, 0:n, 0:np_]),
                       reads=["psb%d" % h], writes=[dstres])

        def proj(nT, nTres, np_, w_t, wres, c0):
            bk = sc.bank()
            for kc in range(8):
                op("pe", lambda e, kc=kc, bk=bk: e.matmul(PS[0:np_, bk, :], lhsT=nT[:, kc, 0:np_], rhs=w_t[:, kc, c0:c0 + 512],
                                                          start=(kc == 0), stop=(kc == 7)),
                   reads=[nTres] + list(wres), writes=["ps%d" % bk], sig=(kc == 7))
            return bk

        def rope_a(bk, np_, tab_ap, tabres, out_ap, outres, W):
            k = W["ia"] % 2
            W["ia"] += 1
            ra, rr_ = W["ra"][k], "ra%d" % k
            psv = PS[0:np_, bk, :].rearrange("p (a d) -> p a d", a=8)
            ov = out_ap.rearrange("p (a d) -> p a d", a=8)
            cb = tab_ap[:, 0:8].unsqueeze(1).to_broadcast([np_, 8, 8])
            sbc = tab_ap[:, 8:16].unsqueeze(1).to_broadcast([np_, 8, 8])
            rav = ra[0:np_].rearrange("p (s a d) -> p s a d", s=4, a=8)
            pr = "ps%d" % bk
            op("act", lambda e: e.copy(out=ov[:, :, 16:64], in_=psv[:, :, 16:64]), reads=[pr], writes=[outres])
            op("dve", lambda e: e.tensor_tensor(out=rav[:, 0], in0=psv[:, :, 0:8], in1=cb, op=ALU.mult),
               reads=[pr, tabres], writes=[rr_ + "a"])
            op("dve", lambda e: e.tensor_tensor(out=rav[:, 1], in0=psv[:, :, 8:16], in1=sbc, op=ALU.mult),
               reads=[pr, tabres], writes=[rr_ + "b"])
            op("dve", lambda e: e.tensor_tensor(out=rav[:, 2], in0=psv[:, :, 8:16], in1=cb, op=ALU.mult),
               reads=[pr, tabres], writes=[rr_ + "c"])
            op("dve", lambda e: e.tensor_tensor(out=rav[:, 3], in0=psv[:, :, 0:8], in1=sbc, op=ALU.mult),
               reads=[pr, tabres], writes=[rr_ + "d"])
            op("dve", lambda e: e.tensor_tensor(out=ov[:, :, 0:8], in0=rav[:, 0], in1=rav[:, 1], op=ALU.subtract),
               reads=[rr_ + "a", rr_ + "b"], writes=[outres])
            op("dve", lambda e: e.tensor_tensor(out=ov[:, :, 8:16], in0=rav[:, 2], in1=rav[:, 3], op=ALU.add),
               reads=[rr_ + "c", rr_ + "d"], writes=[outres])

        def rope_b(ps_ap, pr, np_, nh, gc_ap, gs_ap, gres, out_ap, outres, W, er=None):
            k = W["ib"] % 2
            W["ib"] += 1
            sq, xr, t1, t2, st = W["sq"][k], W["xr"][k], W["t1"][k], W["t2"][k], W["bst"][k]
            n_ = nh * 64
            kk = "rb%d" % k
            op("act", lambda e: e.activation(out=sq[0:np_, 0:n_], in_=ps_ap, func=AF.Square), reads=[pr], writes=[kk + "sq"])
            op("dve", lambda e: e.tensor_reduce(out=st[0:np_, 0:nh], in_=sq[0:np_, 0:n_].rearrange("p (a d) -> p a d", a=nh),
                                                axis=AX.X, op=ALU.add), reads=[kk + "sq"], writes=[kk + "st"])
            rstd_of(st[0:np_, 0:nh], 64, st[0:np_, 8:8 + nh], st[0:np_, 16:16 + nh], kk + "st")
            xrv = xr[0:np_, 0:n_].rearrange("p (a d) -> p a d", a=nh)
            t1v = t1[0:np_, 0:n_].rearrange("p (a d) -> p a d", a=nh)
            op("dve", lambda e: e.tensor_tensor(out=xrv, in0=ps_ap.rearrange("p (a d) -> p a d", a=nh),
                                                in1=st[0:np_, 16:16 + nh].unsqueeze(2).to_broadcast([np_, nh, 64]), op=ALU.mult),
               reads=[pr, kk + "st"], writes=[kk + "xr"])
            op("pool", lambda e: e.tensor_tensor(out=t1v, in0=xrv, in1=gc_ap.unsqueeze(1).to_broadcast([np_, nh, 64]), op=ALU.mult),
               reads=[kk + "xr", gres], writes=[kk + "t1"])
            x5 = xr[0:np_, 0:n_].rearrange("p (a h q e) -> p a h q e", a=nh, h=2, q=2)
            t5 = t2[0:np_, 0:n_].rearrange("p (a h q e) -> p a h q e", a=nh, h=2, q=2)
            g4 = gs_ap.rearrange("p (h q e) -> p h q e", h=2, q=2)
            for qo, qi in ((0, 1), (1, 0)):
                op("dve", lambda e, qo=qo, qi=qi: e.tensor_tensor(
                    out=t5[:, :, :, qo, :], in0=x5[:, :, :, qi, :],
                    in1=g4[:, :, qo, :].unsqueeze(1).to_broadcast([np_, nh, 2, 16]), op=ALU.mult),
                   reads=[kk + "xr", gres], writes=[kk + "t2"])
            if er is None:
                i0, i1 = t1v, t2[0:np_, 0:n_].rearrange("p (a d) -> p a d", a=nh)
            else:
                i0 = t1[0:np_, 0:n_].rearrange("p (e r d) -> p e r d", e=er[0], r=er[1])
                i1 = t2[0:np_, 0:n_].rearrange("p (e r d) -> p e r d", e=er[0], r=er[1])
            op("pool", lambda e: e.tensor_tensor(out=out_ap, in0=i0, in1=i1, op=ALU.add),
               reads=[kk + "t1", kk + "t2"], writes=[outres])

        def make_tabs(tb_ap, tbres, np_, W):
            k = W["it"] % 2
            W["it"] += 1
            gt = W["gt"][k]
            r = "gt%d" % k
            for i_, (g_, gr) in enumerate(((gq_t, "gq"), (gk_t, "gk"))):
                op("pool", lambda e, i_=i_, g_=g_: e.tensor_tensor(out=gt[0:np_, 2 * i_, :], in0=tb_ap[:, 0:64], in1=g_[0:np_, :],
                                                                  op=ALU.mult), reads=[tbres, gr], writes=[r])
            for i_, (g_, gr) in enumerate(((gqsw, "gqsw"), (gksw, "gksw"))):
                op("pool", lambda e, i_=i_, g_=g_: e.tensor_tensor(out=gt[0:np_, 2 * i_ + 1, :], in0=tb_ap[:, 64:128],
                                                                  in1=g_[0:np_, :], op=ALU.mult), reads=[tbres, gr], writes=[r])
            return gt, r

        for u in range(NU):
            with ExitStack() as ph:
                def pb(name, shape, dt=F32):
                    return ph.enter_context(nc.sbuf_tensor(_uniq(name), list(shape), dt))
                wkv = pb("wkv", [128, 8, 2560], BF16)
                for kc in range(8):
                    dma(wkv[:, kc, 0:2048], Wb_in[kc * 128:(kc + 1) * 128, 1024:3072], "wkv", writes=["wkv%da" % kc], q=("sp", "act")[kc % 2])
                    dma(wkv[:, kc, 2048:2560], Wb_in[kc * 128:(kc + 1) * 128, 4096:4608], "wkv2", writes=["wkv%db" % kc], q=("act", "sp")[kc % 2])
                WKV = tuple("wkv%d%s" % (kc, ab) for kc in range(8) for ab in "ab")
                WQ = tuple("wq%d%s" % (kc, ab) for kc in range(8) for ab in "ab")
                gmix_t = pb("gmix_t", [128, D])
                dma(gmix_t[:], g_mix.partition_broadcast(128), "c_gmix", writes=["gmix"])
                xb_ = [pb("xb%d" % i, [128, D]) for i in range(3)]
                tA = [pb("tA%d" % i, [128, 16]) for i in range(3)]
                tB = [pb("tB%d" % i, [128, 128]) for i in range(3)]
                W = dict(i=0, ia=0, ib=0, it=0, junk=pb("junk", [128, D], BF16),
                         nb=[pb("nb%d" % i, [128, D], BF16) for i in range(2)],
                         st=[pb("nst%d" % i, [128, 4]) for i in range(2)],
                         ra=[pb("ra%d" % i, [128, 256]) for i in range(2)],
                         sq=[pb("sq%d" % i, [128, 512]) for i in range(2)],
                         xr=[pb("xr%d" % i, [128, 512]) for i in range(2)],
                         t1=[pb("t1%d" % i, [128, 512]) for i in range(2)],
                         t2=[pb("t2%d" % i, [128, 512]) for i in range(2)],
                         bst=[pb("bst%d" % i, [128, 24]) for i in range(2)],
                         gt=[pb("gt%d" % i, [128, 4, 64]) for i in range(2)])
                nT = [pb("nT%d" % i, [128, 8, 128], BF16) for i in range(2)]
                ktm = [pb("ktm%d" % i, [128, D], BF16) for i in range(2)]
                vtm = [pb("vtm%d" % i, [128, D], BF16) for i in range(2)]
                bkt = [pb("bkt%d" % i, [128, 256], BF16) for i in range(2)]
                bvt = [pb("bvt%d" % i, [128, 256], BF16) for i in range(2)]
                kst = [pb("kst%d" % i, [128, 8, 128], BF16) for i in range(2)]
                bks = [pb("bks%d" % i, [128, 2, 128], BF16) for i in range(2)]
                nsb = S // 128

                def load1a(s):
                    k = s % 3
                    dma(xb_[k][:], xs[u, s * 128:(s + 1) * 128, :], "xb%d" % k, writes=["xb%d" % k])
                    dma(tA[k][:], tabAk[s * 128:(s + 1) * 128, :], "tA%d" % k, writes=["tA%d" % k])
                    dma(tB[k][:], tabBk[s * 128:(s + 1) * 128, :], "tB%d" % k, writes=["tB%d" % k])

                nkb = {}

                def norm1a(s, part):
                    k3, k2 = s % 3, s % 2
                    r = norm_transpose(xb_[k3][:], "xb%d" % k3, gmix_t, "gmix", 128,
                                       lambda k0, k2=k2: nT[k2][:, k0:k0 + 4, :], "nT%d" % k2, W, part=part, kbuf=nkb.get(s))
                    if part == "stats":
                        nkb[s] = r

                def fin1a(s):
                    k2, t0 = s % 2, s * 128
                    transpose_cols(ktm[k2][:], "ktm%d" % k2, 128, 8,
                                   lambda k0, n, k2=k2: kst[k2][:, k0:k0 + n, :], "kst%d" % k2)
                    dma(KTA[:, :, t0:t0 + 128].rearrange("h p t -> p h t"), kst[k2][:], "kst%d" % k2, reads=["kst%d" % k2])
                    transpose_cols(bkt[k2][:], "bkt%d" % k2, 128, 2,
                                   lambda k0, n, k2=k2: bks[k2][:, k0:k0 + n, :], "bks%d" % k2)
                    dma(KTB[:, :, t0:t0 + 128].rearrange("c p t -> p c t"), bks[k2][:], "bks%d" % k2, reads=["bks%d" % k2])

                load1a(0)
                if nsb > 1:
                    load1a(1)
                norm1a(0, "stats")
                norm1a(0, "tr")
                for s in range(nsb):
                    if s + 2 < nsb:
                        load1a(s + 2)
                    if s + 1 < nsb:
                        norm1a(s + 1, "stats")
                    k3, k2 = s % 3, s % 2
                    t0 = s * 128
                    gt, gtr = make_tabs(tB[k3][:], "tB%d" % k3, 128, W)
                    for cc in range(2):
                        bk = proj(nT[k2], "nT%d" % k2, 128, wkv, WKV, cc * 512)
                        rope_a(bk, 128, tA[k3][:], "tA%d" % k3, ktm[k2][:, cc * 512:(cc + 1) * 512], "ktm%d" % k2, W)
                    bk = proj(nT[k2], "nT%d" % k2, 128, wkv, WKV, 2048)
                    rope_b(PS[:, bk, 0:256], "ps%d" % bk, 128, 4, gt[:, 2, :], gt[:, 3, :], gtr,
                           bkt[k2][:].rearrange("p (a d) -> p a d", a=4), "bkt%d" % k2, W)
                    op("act", lambda e, bk=bk: e.copy(out=bvt[k2][:], in_=PS[:, bk, 256:512]), reads=["ps%d" % bk],
                       writes=["bvt%d" % k2])
                    dma(VB[:, t0:t0 + 128, :].rearrange("g t e -> t g e"), bvt[k2][:].rearrange("p (g e) -> p g e", g=4),
                        "bvt%d" % k2, reads=["bvt%d" % k2])
                    for cc in range(2):
                        bk = proj(nT[k2], "nT%d" % k2, 128, wkv, WKV, 1024 + cc * 512)
                        en = sc.pick("act", "dve")
                        if en == "act":
                            op("act", lambda e, bk=bk, cc=cc: e.copy(out=vtm[k2][:, cc * 512:(cc + 1) * 512], in_=PS[:, bk, :]),
                               reads=["ps%d" % bk], writes=["vtm%d" % k2])
                        else:
                            op("dve", lambda e, bk=bk, cc=cc: e.tensor_copy(out=vtm[k2][:, cc * 512:(cc + 1) * 512], in_=PS[:, bk, :]),
                               reads=["ps%d" % bk], writes=["vtm%d" % k2])
                    dma(VA[:, t0:t0 + 128, :].rearrange("h t e -> t h e"), vtm[k2][:].rearrange("p (h e) -> p h e", h=8),
                        "vtm%d" % k2, reads=["vtm%d" % k2])
                    if s + 1 < nsb:
                        norm1a(s + 1, "tr")
                    if s >= 1:
                        fin1a(s - 1)
                fin1a(nsb - 1)
                sc.barrier()
                if _STOP == 2:
                    return nc

            with ExitStack() as ph:
                def pb(name, shape, dt=F32):
                    return ph.enter_context(nc.sbuf_tensor(_uniq(name), list(shape), dt))
                wq = pb("wq", [128, 8, 2048], BF16)
                for kc in range(8):
                    dma(wq[:, kc, 0:1024], Wb_in[kc * 128:(kc + 1) * 128, 0:1024], "wq", writes=["wq%da" % kc], q=("sp", "act")[kc % 2])
                    dma(wq[:, kc, 1024:2048], Wb_in[kc * 128:(kc + 1) * 128, 3072:4096], "wq2", writes=["wq%db" % kc], q=("act", "sp")[kc % 2])
                WKV = tuple("wkv%d%s" % (kc, ab) for kc in range(8) for ab in "ab")
                WQ = tuple("wq%d%s" % (kc, ab) for kc in range(8) for ab in "ab")
                gmix_t = pb("gmix_t", [128, D])
                dma(gmix_t[:], g_mix.partition_broadcast(128), "c_gmix", writes=["gmix"])
                xb_ = [pb("xb%d" % i, [128, D]) for i in range(3)]
                tA = [pb("tA%d" % i, [128, 16]) for i in range(3)]
                tB = [pb("tB%d" % i, [128, 128]) for i in range(3)]
                W = dict(i=0, ia=0, ib=0, it=0, junk=pb("junk", [128, D], BF16),
                         nb=[pb("nb%d" % i, [128, D], BF16) for i in range(2)],
                         st=[pb("nst%d" % i, [128, 4]) for i in range(2)],
                         ra=[pb("ra%d" % i, [128, 256]) for i in range(2)],
                         sq=[pb("sq%d" % i, [128, 512]) for i in range(2)],
                         xr=[pb("xr%d" % i, [128, 512]) for i in range(2)],
                         t1=[pb("t1%d" % i, [128, 512]) for i in range(2)],
                         t2=[pb("t2%d" % i, [128, 512]) for i in range(2)],
                         bst=[pb("bst%d" % i, [128, 24]) for i in range(2)],
                         gt=[pb("gt%d" % i, [128, 4, 64]) for i in range(2)])
                nT = [pb("nT%d" % i, [128, 8, 128], BF16) for i in range(2)]
                qtm = [pb("qtm%d" % i, [128, D], BF16) for i in range(2)]
                bqt = [pb("bqt%d" % i, [128, D], BF16) for i in range(2)]
                qst = [pb("qst%d" % i, [128, 8, 128], BF16) for i in range(2)]
                qbs = [pb("qbs%d" % i, [128, 8, 128], BF16) for i in range(2)]
                blocks = [(s * 128, 128) for s in range(NSUBQ)] + [(QN, HW)]

                def load1b(i):
                    t0, np_ = blocks[i]
                    k = i % 3
                    dma(xb_[k][0:np_, :], xq[u, t0:t0 + np_, :], "xb%d" % k, writes=["xb%d" % k])
                    dma(tA[k][0:np_, :], tabAq[u, t0:t0 + np_, :], "tA%d" % k, writes=["tA%d" % k])
                    dma(tB[k][0:np_, :], tabBq[u, t0:t0 + np_, :], "tB%d" % k, writes=["tB%d" % k])

                nkb = {}

                def norm1b(i, part):
                    t0, np_ = blocks[i]
                    k3, k2 = i % 3, i % 2
                    r = norm_transpose(xb_[k3][0:np_, :], "xb%d" % k3, gmix_t, "gmix", np_,
                                       lambda k0, k2=k2, np_=np_: nT[k2][:, k0:k0 + 4, 0:np_], "nT%d" % k2, W, part=part, kbuf=nkb.get(i))
                    if part == "stats":
                        nkb[i] = r
                    else:
                        dma(NT[:, :, t0:t0 + np_].rearrange("k p t -> p k t"), nT[k2][:, :, 0:np_], "nT%d" % k2, reads=["nT%d" % k2])

                def fin1b(i):
                    t0, np_ = blocks[i]
                    k2 = i % 2
                    transpose_cols(qtm[k2][0:np_, :], "qtm%d" % k2, np_, 8,
                                   lambda k0, n, k2=k2, np_=np_: qst[k2][:, k0:k0 + n, 0:np_], "qst%d" % k2)
                    dma(QTA[:, :, t0:t0 + np_].rearrange("h p t -> p h t"), qst[k2][:, :, 0:np_], "qst%d" % k2, reads=["qst%d" % k2])
                    transpose_cols(bqt[k2][0:np_, :], "bqt%d" % k2, np_, 8,
                                   lambda k0, n, k2=k2, np_=np_: qbs[k2][:, k0:k0 + n, 0:np_], "qbs%d" % k2)
                    for c in range(2):
                        if np_ == 128:
                            col = (t0 // 128) * 512
                            dma(QTB[c, :, col:col + 512].rearrange("p (r t) -> p r t", r=4), qbs[k2][:, 4 * c:4 * c + 4, :],
                                "qbs%d" % k2, reads=["qbs%d" % k2])
                        else:
                            col = 4 * QN
                            dma(QTB[c, :, col:col + 4 * HW].rearrange("p (r t) -> p r t", r=4), qbs[k2][:, 4 * c:4 * c + 4, 0:HW],
                                "qbs%d" % k2, reads=["qbs%d" % k2])

                load1b(0)
                load1b(1)
                norm1b(0, "stats")
                norm1b(0, "tr")
                for i, (t0, np_) in enumerate(blocks):
                    if i + 2 < len(blocks):
                        load1b(i + 2)
                    if i + 1 < len(blocks):
                        norm1b(i + 1, "stats")
                    k3, k2 = i % 3, i % 2
                    gt, gtr = make_tabs(tB[k3][0:np_, :], "tB%d" % k3, np_, W)
                    for cc in range(2):
                        bk = proj(nT[k2], "nT%d" % k2, np_, wq, WQ, cc * 512)
                        rope_a(bk, np_, tA[k3][0:np_, :], "tA%d" % k3, qtm[k2][0:np_, cc * 512:(cc + 1) * 512], "qtm%d" % k2, W)
                    for c in range(2):
                        bk = proj(nT[k2], "nT%d" % k2, np_, wq, WQ, 1024 + c * 512)
                        ov = bqt[k2][0:np_, c * 512:(c + 1) * 512].rearrange("p (r e d) -> p e r d", r=4, e=2)
                        rope_b(PS[0:np_, bk, :], "ps%d" % bk, np_, 8, gt[0:np_, 0, :], gt[0:np_, 1, :], gtr,
                               ov, "bqt%d" % k2, W, er=(2, 4))
                    if i + 1 < len(blocks):
                        norm1b(i + 1, "tr")
                    if i >= 1:
                        fin1b(i - 1)
                fin1b(len(blocks) - 1)
                sc.barrier()
                if _STOP == 3:
                    return nc

            with ExitStack() as ph:
                def pb(name, shape, dt=F32):
                    return ph.enter_context(nc.sbuf_tensor(_uniq(name), list(shape), dt))
                KT = [pb("KT%d" % i, [128, S], BF16) for i in range(2)]
                VtA = [pb("VtA%d" % i, [128, NKB, 128], BF16) for i in range(2)]
                VtB = [pb("VtB%d" % i, [128, 2, NKB, 64], BF16) for i in range(1)]
                QtA = [pb("QtA%d" % i, [128, QT], BF16) for i in range(2)]
                QtB = [pb("QtB%d" % i, [128, 4 * QT], BF16) for i in range(1)]
                NPT = 4
                PT = [pb("PT%d" % i, [128, 2, 512], BF16) for i in range(NPT)]
                P2 = [pb("P2%d" % i, [128, 2, 512], BF16) for i in range(2)]
                P4 = [pb("P4%d" % i, [128, 2, 512], BF16) for i in range(2)]
                osb = [pb("osb%d" % i, [128, 2, 512]) for i in range(2)]
                rl = [pb("rl%d" % i, [128, 2, 512]) for i in range(2)]
                pa = [pb("pa%d" % i, [128, 2, 512]) for i in range(2)]
                oab = [pb("oab%d" % i, [128, 2, 512], BF16) for i in range(2)]

                def BK(i):
                    return PS[:, i, :] if i < 6 else PSB[:, i - 6, :].bitcast(F32)

                jobs = [("A", h) for h in range(8)] + [("B", c) for c in range(2)]

                def load_job(ji, part):
                    kind, idx = jobs[ji]
                    k = ji % 2
                    if kind == "A":
                        if part == 1:
                            return
                        dma(KT[k][:], KTA[idx], "KT%d" % k, writes=["KT%d" % k])
                        dma(VtA[k][:], VA[idx].rearrange("(kb p) e -> p kb e", p=128), "VtA%d" % k, writes=["VtA%d" % k])
                        dma(QtA[k][:], QTA[idx], "QtA%d" % k, writes=["QtA%d" % k])
                    elif part == 0:
                        dma(KT[k][:], KTB[idx], "KT%d" % k, writes=["KT%d" % k])
                    else:
                        for m in range(2):
                            dma(VtB[0][:, m, :, :], VB[2 * idx + m].rearrange("(kb p) e -> p kb e", p=128), "VtB0", writes=["VtB0"])
                        dma(QtB[0][:], QTB[idx], "QtB0", writes=["QtB0"])

                cnt = dict(step=0, post=0)

                def attention(kt, ktr, q_ap, qr, v_fn, vr, dv, col0, Wd, pre=None):
                    base = cnt["step"]
                    cnt["step"] += NKB

                    def scores(kb):
                        b_ = (base + kb) % 2
                        for m in range(2):
                            op("pe", lambda e, m=m: e.matmul(
                                PS[:, 2 * b_ + m, 0:Wd], lhsT=kt[64 * m:64 * m + 64, kb * 128:(kb + 1) * 128],
                                rhs=q_ap[64 * m:64 * m + 64, col0:col0 + Wd], start=True, stop=True),
                               reads=[ktr, qr], writes=["S%d" % b_], sig=(m == 1))

                    def expo(kb):
                        b_ = (base + kb) % 2
                        p_ = (base + kb) % NPT
                        op("act", lambda e: e.activation(out=PT[p_][:, :, 0:Wd], in_=PS[:, 2 * b_:2 * b_ + 2, 0:Wd],
                                                         func=AF.Exp, scale=0.125),
                           reads=["S%d" % b_], writes=["PT%d" % p_])

                    def psum4(kb):
                        if kb % 2 == 1:
                            a_, b2 = (base + kb - 1) % NPT, (base + kb) % NPT
                            j2 = (kb // 2) % 2
                            op("dve", lambda e: e.tensor_tensor(out=P2[j2][:, :, 0:Wd], in0=PT[a_][:, :, 0:Wd], in1=PT[b2][:, :, 0:Wd],
                                                                op=ALU.add), reads=["PT%d" % a_, "PT%d" % b2], writes=["P2%d" % j2])
                        if kb % 4 == 3 and kb != NKB - 1:
                            j4 = (kb // 4) % 2
                            op("dve", lambda e: e.tensor_tensor(out=P4[j4][:, :, 0:Wd], in0=P2[0][:, :, 0:Wd], in1=P2[1][:, :, 0:Wd],
                                                                op=ALU.add), reads=["P20", "P21"], writes=["P4%d" % j4])

                    def av(kb):
                        b_ = (base + kb) % NPT
                        for m in range(2):
                            op("pe", lambda e, m=m: e.matmul(BK(4 + m)[0:dv, 0:Wd], lhsT=v_fn(m, kb), rhs=PT[b_][:, m, 0:Wd],
                                                             start=(kb == 0), stop=(kb == NKB - 1)),
                               reads=["PT%d" % b_, vr], writes=["Ob%d" % (4 + m)], sig=(m == 1))
                        if kb % 4 == 0 and kb >= 4:
                            g_ = kb // 4 - 1
                            for m in range(2):
                                op("pe", lambda e, m=m: e.matmul(BK(6 + m)[0:dv, 0:Wd], lhsT=onesb[:, 0:dv], rhs=P4[g_ % 2][:, m, 0:Wd],
                                                                 start=(g_ == 0), stop=False),
                                   reads=["P4%d" % (g_ % 2), "onesb"], writes=["Ob%d" % (6 + m)], sig=(m == 1))
                        if kb == NKB - 1:
                            for j2 in range(2):
                                for m in range(2):
                                    op("pe", lambda e, m=m, j2=j2: e.matmul(BK(6 + m)[0:dv, 0:Wd], lhsT=onesb[:, 0:dv], rhs=P2[j2][:, m, 0:Wd],
                                                                            start=(NKB == 4 and j2 == 0), stop=(j2 == 1)),
                                       reads=["P2%d" % j2, "onesb"], writes=["Ob%d" % (6 + m)], sig=(m == 1))

                    scores(0)
                    for kb in range(NKB):
                        if kb + 1 < NKB:
                            scores(kb + 1)
                        expo(kb)
                        psum4(kb)
                        if pre and (kb == 1 or (kb >= 3 and kb % 3 == 0)):
                            pre.pop(0)()
                        if kb >= 1:
                            av(kb - 1)
                    while pre:
                        pre.pop(0)()
                    av(NKB - 1)

                def post(kind, idx, dv, Wd, tok0):
                    k = cnt["post"] % 2
                    cnt["post"] += 1
                    st = []

                    def evac():
                        for m in range(2):
                            op("act", lambda e, m=m: e.copy(out=rl[k][0:dv, m, 0:Wd], in_=BK(6 + m)[0:dv, 0:Wd]),
                               reads=["Ob%d" % (6 + m)], writes=["rl%d_%d" % (k, m)])
                            op("dve", lambda e, m=m: e.tensor_copy(out=osb[k][0:dv, m, 0:Wd], in_=BK(4 + m)[0:dv, 0:Wd]),
                               reads=["Ob%d" % (4 + m)], writes=["osb%d_%d" % (k, m)])
                    st.append(evac)
                    for m in range(2):
                        for c0 in range(0, Wd, 128):
                            c1 = min(Wd, c0 + 128)
                            st.append(lambda m=m, c0=c0, c1=c1: op(
                                "dve", lambda e: e.reciprocal(out=rl[k][0:dv, m, c0:c1], in_=rl[k][0:dv, m, c0:c1]),
                                reads=["rl%d_%d" % (k, m)], writes=["rl%d_%d" % (k, m)]))
                    RL = ["rl%d_0" % k, "rl%d_1" % k]

                    def fin():
                        if kind == "A":
                            op("pool", lambda e: e.tensor_tensor(out=pa[k][:, :, 0:Wd], in0=osb[k][:, :, 0:Wd], in1=rl[k][:, :, 0:Wd],
                                                                 op=ALU.mult), reads=["osb%d_0" % k, "osb%d_1" % k] + RL, writes=["pa%d" % k])
                            op("dve", lambda e: e.scalar_tensor_tensor(out=oab[k][:, 0, 0:Wd], in0=pa[k][:, 1, 0:Wd], scalar=lams[:, 6:7],
                                                                       in1=pa[k][:, 0, 0:Wd], op0=ALU.mult, op1=ALU.add),
                               reads=["pa%d" % k, "lams"], writes=["oab%d" % k])
                            dma(OT[idx, :, tok0:tok0 + Wd], oab[k][:, 0, 0:Wd], "oab%d" % k, reads=["oab%d" % k])
                        else:
                            op("pool", lambda e: e.tensor_tensor(out=oab[k][0:64, :, 0:Wd], in0=osb[k][0:64, :, 0:Wd],
                                                                 in1=rl[k][0:64, :, 0:Wd], op=ALU.mult),
                               reads=["osb%d_0" % k, "osb%d_1" % k] + RL, writes=["oab%d" % k])
                            OTf = OT.rearrange("c p t -> (c p) t")
                            wq_ = Wd // 4
                            for m in range(2):
                                g = 2 * idx + m
                                dma(OTf[1024 + g * 256:1024 + (g + 1) * 256, tok0:tok0 + wq_].rearrange("(r d) q -> d r q", r=4),
                                    oab[k][0:64, m, 0:Wd].rearrange("d (r q) -> d r q", r=4), "oab%d" % k, reads=["oab%d" % k])
                    st.append(fin)
                    return st

                pend = [None]
                load_job(0, 0)
                for ji, (kind, idx) in enumerate(jobs):
                    if ji + 1 < len(jobs):
                        load_job(ji + 1, 0)
                        if kind == "A":
                            load_job(ji + 1, 1)
                    if kind == "B" and ji > 0 and jobs[ji - 1][0] == "B":
                        load_job(ji, 1)
                    k = ji % 2
                    if kind == "A":
                        qbl = [(b * 512, 512, b * 512) for b in range(QN // 512)] + [(QN, HW, QN)]
                        for (col0, Wd, tok0) in qbl:
                            attention(KT[k], "KT%d" % k, QtA[k], "QtA%d" % k, lambda m, kb, k=k: VtA[k][:, kb, :], "VtA%d" % k, 128, col0, Wd,
                                      pre=pend[0])
                            pend[0] = post("A", idx, 128, Wd, tok0)
                    else:
                        qbl = [(s_ * 512, 512, s_ * 128) for s_ in range(NSUBQ)] + [(4 * QN, 4 * HW, QN)]
                        for (col0, Wd, tok0) in qbl:
                            attention(KT[k], "KT%d" % k, QtB[0], "QtB0", lambda m, kb: VtB[0][:, m, kb, :], "VtB0", 64, col0, Wd,
                                      pre=pend[0])
                            pend[0] = post("B", idx, 64, Wd, tok0)
                while pend[0]:
                    pend[0].pop(0)()
                sc.barrier()
                if _STOP == 4:
                    return nc

            with ExitStack() as ph3:
                n2T = ph3.enter_context(nc.sbuf_tensor(_uniq("n2T"), [128, 8, QT], BF16))
                with ExitStack() as ph:
                    def pb(name, shape, dt=F32):
                        return ph.enter_context(nc.sbuf_tensor(_uniq(name), list(shape), dt))
                    wa = pb("wa", [128, 8, D], BF16)
                    wb = pb("wb", [128, 8, D], BF16)
                    wo = pb("wo", [128, 8, D], BF16)
                    wg = pb("wg", [128, 8, 2048], BF16)
                    for kc in range(8):
                        r = slice(kc * 128, (kc + 1) * 128)
                        dma(wa[:, kc, :], Wb_a[r, :], "wa", writes=["wa%d" % kc], q="sp")
                        dma(wb[:, kc, :], Wb_b[r, :], "wb", writes=["wb%d" % kc], q="act")
                        dma(wg[:, kc, :], Wb_in[r, 4608:6656], "wg", writes=["wg%d" % kc], q="sp")
                        dma(wo[:, kc, :], Wb_out[r, :], "wo", writes=["wo%d" % kc], q="act")
                    gffn_t = pb("gffn_t", [128, D])
                    dma(gffn_t[:], g_ffn.partition_broadcast(128), "c_gffn", writes=["gffn"])
                    TB = 256
                    ota = [pb("ota%d" % i, [128, 8, TB], BF16) for i in range(2)]
                    otb = [pb("otb%d" % i, [128, 8, TB], BF16) for i in range(2)]
                    ntg = [pb("ntg%d" % i, [128, 8, TB], BF16) for i in range(2)]
                    xh = [pb("xh%d" % i, [128, 2, D]) for i in range(2)]
                    mT = pb("mT", [128, 8, TB], BF16)
                    sq8 = pb("sq8", [128, 8, TB], BF16)
                    rt8 = pb("rt8", [128, 8, TB])
                    sg = [pb("sg%d" % i, [128, 2, 256]) for i in range(2)]
                    tt = [pb("tt%d" % i, [128, 2, 256]) for i in range(2)]
                    W = dict(i=0, junk=pb("junk", [128, D], BF16),
                             nb=[pb("nb%d" % i, [128, D], BF16) for i in range(2)],
                             st=[pb("nst%d" % i, [128, 4]) for i in range(2)])
                    blocks = [(b * TB, TB, TB // 128, 128) for b in range(QN // TB)] + [(QN, HW, 1, HW)]

                    def load3a(i):
                        t0, Wd, nsub, wsub = blocks[i]
                        k = i % 2
                        dma(ota[k][:, :, 0:Wd], OT[0:8, :, t0:t0 + Wd].rearrange("k p t -> p k t"), "ota%d" % k, writes=["ota%d" % k])
                        dma(otb[k][:, :, 0:Wd], OT[8:16, :, t0:t0 + Wd].rearrange("k p t -> p k t"), "otb%d" % k, writes=["otb%d" % k])
                        dma(ntg[k][:, :, 0:Wd], NT[:, :, t0:t0 + Wd].rearrange("k p t -> p k t"), "ntg%d" % k, writes=["ntg%d" % k])
                        dma(xh[k][0:wsub, 0:nsub, :], xq[u, t0:t0 + Wd, :].rearrange("(j p) d -> p j d", p=wsub), "xh%d" % k,
                            writes=["xh%d" % k])

                    def prep3a(i):
                        t0, Wd, nsub, wsub = blocks[i]
                        k = i % 2
                        op("pool", lambda e: e.tensor_tensor(out=sq8[:, :, 0:Wd], in0=ota[k][:, :, 0:Wd], in1=ota[k][:, :, 0:Wd], op=ALU.mult),
                           reads=["ota%d" % k], writes=["sq8"])
                        bks_ = []
                        for hp in range(4):
                            bk = sc.bank()
                            bks_.append(bk)
                            for z in range(2):
                                op("pe", lambda e, bk=bk, z=z, hp=hp: e.matmul(PS[:, bk, z * 256:z * 256 + Wd], lhsT=onesb[:], rhs=sq8[:, 2 * hp + z, 0:Wd],
                                                                              start=True, stop=True, skip_group_check=True),
                                   reads=["sq8", "onesb"], writes=["ps%d" % bk], sig=(z == 1))
                        for hp in range(4):
                            bk = bks_[hp]
                            op("act", lambda e, bk=bk, hp=hp: e.activation(out=rt8[:, 2 * hp:2 * hp + 2, 0:Wd],
                                                                          in_=PS[:, bk, :].rearrange("p (a b) -> p a b", a=2)[:, :, 0:Wd],
                                                                          func=AF.Sqrt, scale=1.0 / 128, bias=EPS),
                               reads=["ps%d" % bk], writes=["rt8_%d" % hp])
                            op("dve", lambda e, hp=hp: e.reciprocal(out=rt8[:, 2 * hp:2 * hp + 2, 0:Wd], in_=rt8[:, 2 * hp:2 * hp + 2, 0:Wd]),
                               reads=["rt8_%d" % hp], writes=["rt8_%d" % hp])
                            for z in range(2):
                                h = 2 * hp + z
                                op("dve", lambda e, h=h: e.scalar_tensor_tensor(out=ota[k][:, h, 0:Wd], in0=ota[k][:, h, 0:Wd], scalar=gdc[:, 0:1],
                                                                               in1=rt8[:, h, 0:Wd], op0=ALU.mult, op1=ALU.mult),
                                   reads=["ota%d" % k, "rt8_%d" % hp, "gdc"], writes=["ota%d" % k])

                    load3a(0)
                    prep3a(0)
                    for i, (t0, Wd, nsub, wsub) in enumerate(blocks):
                        if i + 1 < len(blocks):
                            load3a(i + 1)
                        k = i % 2
                        for oc in range(8):
                            q2 = oc % 2
                            bX, bY = sc.bank(), sc.bank()
                            for gi, (wt, wr, c0, rt, rr_, bk, off) in enumerate(((wa, "wa", 0, ota[k], "ota%d" % k, bX, 0), (wb, "wb", 0, otb[k], "otb%d" % k, bX, 256),
                                                                                 (wg, "wg", 0, ntg[k], "ntg%d" % k, bY, 0), (wg, "wg", 1024, ntg[k], "ntg%d" % k, bY, 256))):
                                for kc in range(8):
                                    op("pe", lambda e, kc=kc, bk=bk, wt=wt, c0=c0, rt=rt, off=off: e.matmul(
                                        PS[:, bk, off:off + Wd], lhsT=wt[:, kc, c0 + oc * 128:c0 + (oc + 1) * 128], rhs=rt[:, kc, 0:Wd],
                                        start=(kc == 0), stop=(kc == 7), skip_group_check=True),
                                       reads=[wr + "%d" % kc, rr_], writes=["ps%d" % bk], sig=(kc == 7))
                            op("act", lambda e, bY=bY, q2=q2: e.activation(out=sg[q2][:, :, 0:Wd], in_=PS[:, bY, :].rearrange("p (a b) -> p a b", a=2)[:, :, 0:Wd],
                                                                          func=AF.Sigmoid), reads=["ps%d" % bY], writes=["sg%d" % q2])
                            op("dve", lambda e, bX=bX, q2=q2: e.tensor_tensor(out=tt[q2][:, :, 0:Wd], in0=PS[:, bX, :].rearrange("p (a b) -> p a b", a=2)[:, :, 0:Wd],
                                                                             in1=sg[q2][:, :, 0:Wd], op=ALU.mult),
                               reads=["ps%d" % bX, "sg%d" % q2], writes=["tt%d" % q2])
                            op("pool", lambda e, q2=q2, oc=oc: e.tensor_tensor(out=mT[:, oc, 0:Wd], in0=tt[q2][:, 0, 0:Wd], in1=tt[q2][:, 1, 0:Wd], op=ALU.add),
                               reads=["tt%d" % q2], writes=["mT"])
                        if i + 1 < len(blocks):
                            prep3a(i + 1)
                        for j in range(nsub):
                            for hf_ in range(2):
                                bk = sc.bank()
                                for kc in range(8):
                                    op("pe", lambda e, kc=kc, bk=bk, j=j, hf_=hf_: e.matmul(
                                        PS[0:wsub, bk, :], lhsT=mT[:, kc, j * wsub:(j + 1) * wsub], rhs=wo[:, kc, hf_ * 512:(hf_ + 1) * 512],
                                        start=(kc == 0), stop=(kc == 7)), reads=["mT", "wo%d" % kc], writes=["ps%d" % bk], sig=(kc == 7))
                                op("dve", lambda e, bk=bk, j=j, hf_=hf_: e.tensor_tensor(
                                    out=xh[k][0:wsub, j, hf_ * 512:(hf_ + 1) * 512], in0=PS[0:wsub, bk, :],
                                    in1=xh[k][0:wsub, j, hf_ * 512:(hf_ + 1) * 512], op=ALU.add),
                                   reads=["ps%d" % bk, "xh%d" % k], writes=["xh%d" % k])
                            if wsub == 128:
                                dma(H2[t0 + j * 128:t0 + (j + 1) * 128, :], xh[k][:, j, :], "xh%d" % k, reads=["xh%d" % k])
                            tt0 = t0 + j * wsub
                            norm_transpose(xh[k][0:wsub, j, :], "xh%d" % k, gffn_t, "gffn", wsub,
                                           lambda k0, tt0=tt0, wsub=wsub: n2T[:, k0:k0 + 4, tt0:tt0 + wsub], "n2T", W)
                    sc.barrier()
                    if _STOP == 5:
                        return nc

                with ExitStack() as ph:
                    def pb(name, shape, dt=F32):
                        return ph.enter_context(nc.sbuf_tensor(_uniq(name), list(shape), dt))
                    wup = [pb("wup%d" % i, [128, 8, 2, 128], BF16) for i in range(2)]
                    U = [pb("U%d" % i, [128, QN + 2]) for i in range(2)]
                    cv = [pb("cv%d" % i, [128, QN]) for i in range(2)]
                    G = pb("G", [128, QN])
                    t2c = [pb("t2c%d" % i, [128, QN]) for i in range(2)]
                    actb = [pb("actb%d" % i, [128, QN], BF16) for i in range(2)]

                    def load3b(fc):
                        k = fc % 2
                        for br in range(2):
                            c0 = br * DFF + fc * 128
                            dma(wup[k][:, :, br, :], Wb_up[:, c0:c0 + 128].rearrange("(k p) f -> p k f", p=128), "wup%d" % k,
                                writes=["wup%d" % k])

                    load3b(0)
                    for fc in range(NFC):
                        if fc + 1 < NFC:
                            load3b(fc + 1)
                        k = fc % 2
                        for br in range(2):
                            cch = fc + NFC * br
                            for blk in range(QN // 512):
                                bk = sc.bank()
                                for kc in range(8):
                                    op("pe", lambda e, kc=kc, bk=bk, blk=blk, br=br: e.matmul(
                                        PS[:, bk, :], lhsT=wup[k][:, kc, br, :], rhs=n2T[:, kc, blk * 512:(blk + 1) * 512],
                                        start=(kc == 0), stop=(kc == 7)), reads=["wup%d" % k, "n2T"], writes=["ps%d" % bk], sig=(kc == 7))
                                op("act", lambda e, bk=bk, blk=blk, br=br: e.copy(out=U[br][:, 1 + blk * 512:1 + (blk + 1) * 512], in_=PS[:, bk, :]),
                                   reads=["ps%d" % bk], writes=["U%d" % br])
                            bk = sc.bank()
                            for kc in range(8):
                                op("pe", lambda e, kc=kc, bk=bk, br=br: e.matmul(
                                    PS[:, bk, 0:HW], lhsT=wup[k][:, kc, br, :], rhs=n2T[:, kc, QN:QN + HW],
                                    start=(kc == 0), stop=(kc == 7)), reads=["wup%d" % k, "n2T"], writes=["ps%d" % bk], sig=(kc == 7))
                            op("dve", lambda e, bk=bk, br=br: e.tensor_scalar(out=U[br][:, 0:1], in0=PS[:, bk, 0:1], scalar1=hm_t[:, 2 * u:2 * u + 1],
                                                                             scalar2=None, op0=ALU.mult),
                               reads=["ps%d" % bk, "hm"], writes=["U%d" % br])
                            op("dve", lambda e, bk=bk, br=br: e.tensor_scalar(out=U[br][:, QN + 1:QN + 2], in0=PS[:, bk, 1:2],
                                                                             scalar1=hm_t[:, 2 * u + 1:2 * u + 2], scalar2=None, op0=ALU.mult),
                               reads=["ps%d" % bk, "hm"], writes=["U%d" % br])
                            op("act", lambda e, br=br, cch=cch: e.activation(out=cv[br][:], in_=U[br][:, 0:QN], func=AF.Copy,
                                                                            scale=cwt[:, 0, cch:cch + 1]),
                               reads=["U%d" % br, "cwt"], writes=["cv%d" % br])
                            op("act", lambda e, br=br, cch=cch: e.activation(out=t2c[br][:], in_=U[br][:, 2:QN + 2], func=AF.Copy,
                                                                            scale=cwt[:, 2, cch:cch + 1]),
                               reads=["U%d" % br, "cwt"], writes=["t2c%d" % br])
                            op("dve", lambda e, br=br, cch=cch: e.scalar_tensor_tensor(
                                out=cv[br][:], in0=U[br][:, 1:1 + QN], scalar=cwt[:, 1, cch:cch + 1], in1=cv[br][:],
                                op0=ALU.mult, op1=ALU.add), reads=["U%d" % br, "cwt", "cv%d" % br], writes=["cv%d" % br])
                            op("pool", lambda e, br=br: e.tensor_tensor(out=cv[br][:], in0=cv[br][:], in1=t2c[br][:], op=ALU.add),
                               reads=["cv%d" % br, "t2c%d" % br], writes=["cv%d" % br])
                        op("act", lambda e, fc=fc: e.activation(out=G[:], in_=cv[0][:], func=AF.Gelu_apprx_tanh, bias=cwt[:, 3, fc:fc + 1]),
                           reads=["cv0", "cwt"], writes=["G"])
                        op("dve", lambda e, fc=fc: e.scalar_tensor_tensor(out=actb[k][:], in0=cv[1][:], scalar=cwt[:, 3, NFC + fc:NFC + fc + 1],
                                                                          in1=G[:], op0=ALU.add, op1=ALU.mult),
                           reads=["cv1", "cwt", "G"], writes=["actb%d" % k])
                        dma(ACTS[fc], actb[k][:], "actb%d" % k, reads=["actb%d" % k])
                    sc.barrier()
                    if _STOP == 6:
                        return nc

            with ExitStack() as ph:
                def pb(name, shape, dt=F32):
                    return ph.enter_context(nc.sbuf_tensor(_uniq(name), list(shape), dt))
                wd = pb("wd", [128, NFC, D], BF16)
                wpg = pb("wpg", [128, 8, D], BF16)
                wpl = pb("wpl", [128, 2, D], BF16)
                for f in range(NFC):
                    dma(wd[:, f, :], Wb_down[f * 128:(f + 1) * 128, :], "wd%d" % (f % 2), writes=["wd%d" % f], q=("sp", "act")[f % 2])
                for kc in range(8):
                    dma(wpg[:, kc, :], Wb_pg[kc * 128:(kc + 1) * 128, :], "wpg%d" % (kc % 2), writes=["wpg%d" % kc], q=("act", "sp")[kc % 2])
                for kc in range(2):
                    dma(wpl[:, kc, :], Wb_ple[kc * 128:(kc + 1) * 128, :], "wpl", writes=["wpl%d" % kc])
                gple_t = pb("gple_t", [128, D])
                gfin_t = pb("gfin_t", [128, D])
                dma(gple_t[:], g_ple.partition_broadcast(128), "c_gple", writes=["gple"])
                dma(gfin_t[:], g_final.partition_broadcast(128), "c_gfin", writes=["gfin"])
                at = [pb("at%d" % i, [128, NFC, 128], BF16) for i in range(2)]
                h2t = [pb("h2t%d" % i, [128, D]) for i in range(2)]
                pt = [pb("pt%d" % i, [128, 256]) for i in range(2)]
                ptb = pb("ptb", [128, 256], BF16)
                pT = pb("pT", [128, 2, 128], BF16)
                n3T = pb("n3T", [128, 8, 128], BF16)
                sgt = pb("sgt", [128, D])
                t4 = pb("t4", [128, D])
                ot = [pb("ot%d" % i, [128, D]) for i in range(2)]
                fst = pb("fst", [128, 4])
                W = dict(i=0, junk=pb("junk", [128, D], BF16),
                         nb=[pb("nb%d" % i, [128, D], BF16) for i in range(2)],
                         st=[pb("nst%d" % i, [128, 4]) for i in range(2)])

                def load3c(s):
                    k = s % 2
                    dma(at[k][:], ACTS[:, :, s * 128:(s + 1) * 128].rearrange("f p t -> p f t"), "at%d" % k, writes=["at%d" % k])
                    dma(h2t[k][:], H2[s * 128:(s + 1) * 128, :], "h2t%d" % k, writes=["h2t%d" % k])
                    dma(pt[k][:], pq[u, s * 128:(s + 1) * 128, :], "pt%d" % k, writes=["pt%d" % k])

                load3c(0)
                for s in range(NSUBQ):
                    if s + 1 < NSUBQ:
                        load3c(s + 1)
                    k = s % 2
                    for hf_ in range(2):
                        bk = sc.bank()
                        for f in range(NFC):
                            op("pe", lambda e, f=f, bk=bk, hf_=hf_: e.matmul(PS[:, bk, :], lhsT=at[k][:, f, :], rhs=wd[:, f, hf_ * 512:(hf_ + 1) * 512],
                                                                            start=(f == 0), stop=(f == NFC - 1)),
                               reads=["at%d" % k, "wd%d" % f], writes=["ps%d" % bk], sig=(f == NFC - 1))
                        op("dve", lambda e, bk=bk, hf_=hf_: e.tensor_tensor(out=h2t[k][:, hf_ * 512:(hf_ + 1) * 512], in0=PS[:, bk, :],
                                                                           in1=h2t[k][:, hf_ * 512:(hf_ + 1) * 512], op=ALU.add),
                           reads=["ps%d" % bk, "h2t%d" % k], writes=["h2t%d" % k])
                    norm_transpose(h2t[k][:], "h2t%d" % k, gple_t, "gple", 128, lambda k0: n3T[:, k0:k0 + 4, :], "n3T", W)
                    op("act", lambda e: e.copy(out=ptb[:], in_=pt[k][:]), reads=["pt%d" % k], writes=["ptb"])
                    transpose_cols(ptb[:], "ptb", 128, 2, lambda k0, n: pT[:, k0:k0 + n, :], "pT")
                    gb_, pb_ = [], []
                    for hf_ in range(2):
                        bk = sc.bank()
                        gb_.append(bk)
                        for kc in range(8):
                            op("pe", lambda e, kc=kc, bk=bk, hf_=hf_: e.matmul(PS[:, bk, :], lhsT=n3T[:, kc, :], rhs=wpg[:, kc, hf_ * 512:(hf_ + 1) * 512],
                                                                              start=(kc == 0), stop=(kc == 7)),
                               reads=["n3T", "wpg%d" % kc], writes=["ps%d" % bk], sig=(kc == 7))
                        bk = sc.bank()
                        pb_.append(bk)
                        for kc in range(2):
                            op("pe", lambda e, kc=kc, bk=bk, hf_=hf_: e.matmul(PS[:, bk, :], lhsT=pT[:, kc, :], rhs=wpl[:, kc, hf_ * 512:(hf_ + 1) * 512],
                                                                              start=(kc == 0), stop=(kc == 1)),
                               reads=["pT", "wpl%d" % kc], writes=["ps%d" % bk], sig=(kc == 1))
                    for hf_ in range(2):
                        cs = slice(hf_ * 512, (hf_ + 1) * 512)
                        op("act", lambda e, hf_=hf_, cs=cs: e.activation(out=sgt[:, cs], in_=PS[:, gb_[hf_], :], func=AF.Sigmoid),
                           reads=["ps%d" % gb_[hf_]], writes=["sgt%d" % hf_])
                        op("dve", lambda e, hf_=hf_, cs=cs: e.tensor_tensor(out=t4[:, cs], in0=PS[:, pb_[hf_], :], in1=sgt[:, cs], op=ALU.mult),
                           reads=["ps%d" % pb_[hf_], "sgt%d" % hf_], writes=["t4"])
                    op("pool", lambda e: e.tensor_tensor(out=t4[:], in0=t4[:], in1=h2t[k][:], op=ALU.add),
                       reads=["t4", "h2t%d" % k], writes=["t4"])
                    op("act", lambda e: e.activation(out=W["junk"][:], in_=t4[:], func=AF.Square, accum_out=fst[:, 0:1]),
                       reads=["t4"], writes=["junk", "fst"])
                    rstd_of(fst[:, 0:1], D, fst[:, 1:2], fst[:, 2:3], "fst")
                    op("dve", lambda e: e.scalar_tensor_tensor(out=ot[k][:], in0=t4[:], scalar=fst[:, 2:3], in1=gfin_t[:],
                                                               op0=ALU.mult, op1=ALU.mult),
                       reads=["t4", "fst", "gfin"], writes=["ot%d" % k])
                    dma(yout[u, s * 128:(s + 1) * 128, :], ot[k][:], "ot%d" % k, reads=["ot%d" % k])
                sc.barrier()
                sc.new_epoch()
                if _STOP == 7:
                    return nc
    return nc


def _rope_tab(pos, dim, theta):
    inv = np.power(np.float32(theta), -(np.arange(0, dim, 2, dtype=np.float32) / np.float32(dim))).astype(np.float32)
    ang = (pos.astype(np.float32)[:, None] * inv[None, :]).astype(np.float32)
    return np.cos(ang.astype(np.float64)).astype(np.float32), np.sin(ang.astype(np.float64)).astype(np.float32)


def _tables(pos):
    ca, sa = _rope_tab(pos, 16, 500000.0)
    cr, sr = _rope_tab(pos // 64, 32, 10000.0)
    cc, sc_ = _rope_tab(pos % 64, 32, 10000.0)
    tA = np.concatenate([ca, sa], axis=1).astype(np.float32)
    tB = np.concatenate([cr, cr, cc, cc, -sr, sr, -sc_, sc_], axis=1).astype(np.float32)
    return np.ascontiguousarray(tA), np.ascontiguousarray(tB)


def run_layer(seqs_x, seqs_p, wts, S, QN, assign):
    ncores = len(assign)
    NU = len(assign[0])
    nc = build(S, QN, NU)
    tAk, tBk = _tables(np.arange(S))
    in_maps = []
    for c in range(ncores):
        xs = np.stack([seqs_x[sq] for sq, _ in assign[c]])
        rows, hm = [], []
        for sq, part in assign[c]:
            t0 = part * QN
            hb, ha = t0 - 1, t0 + QN
            r = np.concatenate([np.arange(t0, t0 + QN), [max(hb, 0), min(ha, S - 1)], np.full(HW - 2, max(hb, 0))])
            rows.append(r.astype(np.int64))
            hm += [1.0 if hb >= 0 else 0.0, 1.0 if ha < S else 0.0]
        xq = np.stack([seqs_x[sq][r] for (sq, _), r in zip(assign[c], rows)])
        pq = np.stack([seqs_p[sq][r[:QN]] for (sq, _), r in zip(assign[c], rows)])
        tq = [_tables(r) for r in rows]
        m = dict(xs=xs, xq=xq, pq=pq, tabAk=tAk, tabBk=tBk,
                 tabAq=np.stack([t[0] for t in tq]), tabBq=np.stack([t[1] for t in tq]),
                 hmask=np.asarray([hm], dtype=np.float32), ident=np.eye(128, dtype=np.float32))
        m.update(wts)
        in_maps.append({k: np.ascontiguousarray(v, dtype=np.float32) for k, v in m.items()})
    res = run_bass_kernel_spmd(nc, in_maps, core_ids=list(range(ncores)))
    nseq = seqs_x.shape[0]
    _LAST["res"] = res.results
    out = np.zeros((nseq, S, D), dtype=np.float32)
    for c in range(ncores):
        y = res.results[c]["y"]
        for ui, (sq, part) in enumerate(assign[c]):
            out[sq, part * QN:(part + 1) * QN] = y[ui]
    return out


def _weights(kw):
    w = {}
    for k in ("g_mix", "lambda_q1", "lambda_k1", "lambda_q2", "lambda_k2", "g_diff", "g_qn", "g_kn", "g_ffn", "conv_b", "g_ple"):
        w[k] = np.asarray(kw[k]).reshape(1, -1)
    w["g_final"] = np.asarray(kw["g_final"]).reshape(1, -1)
    for k in ("w_in", "w_a", "w_b", "w_out", "w_up", "conv_w", "w_down", "w_ple", "w_ple_gate"):
        w[k] = np.asarray(kw[k])[0]
    return w


def kernel(**inputs):
    xp = np.asarray(inputs["x_prompt"])
    xsm = np.asarray(inputs["x_sample"])
    pp = np.asarray(inputs["p_prompt"])[0]
    psm = np.asarray(inputs["p_sample"])[0]
    seqs_x = np.concatenate([xp, xsm], axis=0)
    seqs_p = np.concatenate([pp, psm], axis=0)
    S = seqs_x.shape[1]
    QN = S // 4
    assign = [[(3 * (c // 4) + i, c % 4) for i in range(3)] for c in range(8)]
    out = run_layer(seqs_x, seqs_p, _weights(inputs), S, QN, assign)
    nb = xp.shape[0]
    return (np.ascontiguousarray(out[:nb]), np.ascontiguousarray(out[nb:]))
```
